# Optimizing a Trainium2 kernel written in Bass

```python
import math
import jax, jax.numpy as jnp
from jax import lax
import numpy as np

D_MODEL = 2048
BATCH = 4
SEQ = 2048
DEPTH = 2

N_MIXERS = 2
N_SSD_LAYERS = (DEPTH + 1) // 2
N_FOX_LAYERS = DEPTH // 2
EPS = 1e-6

SSD_EXPAND = 2
SSD_D_INNER = SSD_EXPAND * D_MODEL
SSD_HEAD_DIM = 64
SSD_N_HEADS = SSD_D_INNER // SSD_HEAD_DIM
SSD_N_GROUPS = 8
SSD_HEADS_PER_GROUP = SSD_N_HEADS // SSD_N_GROUPS
SSD_D_STATE = 128
SSD_CONV_K = 4
SSD_CHUNK = 128
SSD_CONV_DIM = SSD_D_INNER + 2 * SSD_N_GROUPS * SSD_D_STATE
SSD_IN_DIM = SSD_D_INNER + SSD_CONV_DIM + SSD_N_HEADS

FOX_N_HEADS = 16
FOX_HEAD_DIM = 128
FOX_WIDTH = FOX_N_HEADS * FOX_HEAD_DIM
FOX_IN_DIM = 3 * FOX_WIDTH + FOX_N_HEADS
FOX_Q_BLOCK = 128

FFN_HIDDEN = int(math.ceil(8 * D_MODEL / 3 / 256) * 256)

PLE_DIM = 256

kernel_name = "hybrid_ssd_fox_sandwich_ple"


def rmsnorm(x, w):
    xf = x.astype(jnp.float32)
    y = xf * lax.rsqrt(jnp.mean(xf * xf, axis=-1, keepdims=True) + EPS)
    return (y * w.astype(jnp.float32)).astype(x.dtype)


def causal_depthwise_conv(x, w, b):
    k = w.shape[0]
    out = lax.conv_general_dilated(
        x, w[:, None, :].astype(x.dtype), window_strides=(1,), padding=[(k - 1, 0)],
        dimension_numbers=("NWC", "WIO", "NWC"), feature_group_count=x.shape[-1])
    return out + b.astype(x.dtype)


def ssd_chunked_scan(x, dt, a, bm, cm):
    b, l, g, r, p = x.shape
    n = bm.shape[-1]
    nc = l // SSD_CHUNK
    q = SSD_CHUNK
    x = x.reshape(b, nc, q, g, r, p)
    dt = dt.reshape(b, nc, q, g, r)
    bm = bm.reshape(b, nc, q, g, n)
    cm = cm.reshape(b, nc, q, g, n)
    xdt = x * dt[..., None]
    a_dt = (dt * a).transpose(0, 3, 4, 1, 2)
    a_cum = jnp.cumsum(a_dt, axis=-1)
    seg = a_cum[..., :, None] - a_cum[..., None, :]
    tril = jnp.tril(jnp.ones((q, q), dtype=bool))
    decay_l = jnp.exp(jnp.where(tril, seg, -jnp.inf))
    cb = jnp.einsum("bclgn,bcsgn->bcgls", cm, bm)
    y_diag = jnp.einsum("bcgls,bgrcls,bcsgrp->bclgrp", cb, decay_l, xdt)
    decay_states = jnp.exp(a_cum[..., -1:] - a_cum)
    states = jnp.einsum("bcsgn,bgrcs,bcsgrp->cbgrpn", bm, decay_states, xdt)
    chunk_decay = jnp.moveaxis(jnp.exp(a_cum[..., -1]), -1, 0)

    def step(h, inp):
        st, dec = inp
        return h * dec[..., None, None] + st, h

    h0 = jnp.zeros((b, g, r, p, n), dtype=jnp.float32)
    _, prev = lax.scan(step, h0, (states, chunk_decay))
    y_off = jnp.einsum("bclgn,cbgrpn,bgrcl->bclgrp", cm, prev, jnp.exp(a_cum))
    return (y_diag + y_off).reshape(b, l, g, r, p)


def ssd_mixer(u, w_in, conv_w, conv_b, dt_bias, a_log, d_skip, norm_w, w_out):
    b, l, _ = u.shape
    g, r, pd, n = SSD_N_GROUPS, SSD_HEADS_PER_GROUP, SSD_HEAD_DIM, SSD_D_STATE
    proj = u @ w_in
    z = proj[..., :SSD_D_INNER]
    xbc = proj[..., SSD_D_INNER:SSD_D_INNER + SSD_CONV_DIM]
    dt = proj[..., SSD_D_INNER + SSD_CONV_DIM:]
    xbc = jax.nn.silu(causal_depthwise_conv(xbc, conv_w, conv_b))
    xs = xbc[..., :SSD_D_INNER].astype(jnp.float32).reshape(b, l, g, r, pd)
    bm = xbc[..., SSD_D_INNER:SSD_D_INNER + g * n].astype(jnp.float32).reshape(b, l, g, n)
    cm = xbc[..., SSD_D_INNER + g * n:].astype(jnp.float32).reshape(b, l, g, n)
    dt = jax.nn.softplus(dt.astype(jnp.float32) + dt_bias.astype(jnp.float32)).reshape(b, l, g, r)
    a = -jnp.exp(a_log.astype(jnp.float32)).reshape(g, r)
    y = ssd_chunked_scan(xs, dt, a, bm, cm)
    y = y + xs * d_skip.astype(jnp.float32).reshape(g, r, 1)
    y = y.reshape(b, l, SSD_D_INNER) * jax.nn.silu(z.astype(jnp.float32))
    yg = y.reshape(b, l, g, SSD_D_INNER // g)
    yg = yg * lax.rsqrt(jnp.mean(yg * yg, axis=-1, keepdims=True) + EPS)
    y = yg.reshape(b, l, SSD_D_INNER) * norm_w.astype(jnp.float32)
    return y.astype(u.dtype) @ w_out


def fox_mixer(u, w_in, b_f, w_out):
    b, l, _ = u.shape
    hn, hd = FOX_N_HEADS, FOX_HEAD_DIM
    proj = u @ w_in
    q = proj[..., :FOX_WIDTH].reshape(b, l, hn, hd) * (hd ** -0.5)
    k = proj[..., FOX_WIDTH:2 * FOX_WIDTH].reshape(b, l, hn, hd)
    v = proj[..., 2 * FOX_WIDTH:3 * FOX_WIDTH].reshape(b, l, hn, hd)
    log_f = jax.nn.log_sigmoid(proj[..., 3 * FOX_WIDTH:].astype(jnp.float32) + b_f.astype(jnp.float32))
    csum = jnp.cumsum(log_f, axis=1).transpose(0, 2, 1)
    outs = []
    for blk in range(l // FOX_Q_BLOCK):
        lo = blk * FOX_Q_BLOCK
        hi = lo + FOX_Q_BLOCK
        s = jnp.einsum("bqhd,bkhd->bhqk", q[:, lo:hi], k[:, :hi],
                       preferred_element_type=jnp.float32)
        s = s + csum[:, :, lo:hi, None] - csum[:, :, None, :hi]
        mask = jnp.arange(lo, hi)[:, None] >= jnp.arange(hi)[None, :]
        s = jnp.where(mask, s, -jnp.inf)
        pr = jax.nn.softmax(s, axis=-1).astype(v.dtype)
        outs.append(jnp.einsum("bhqk,bkhd->bqhd", pr, v[:, :hi]))
    o = jnp.concatenate(outs, axis=1).reshape(b, l, FOX_WIDTH)
    return o @ w_out


def swiglu(h, w_gate, w_up, w_down):
    return (jax.nn.silu(h @ w_gate) * (h @ w_up)) @ w_down


def setup_inputs(seed: int = 0) -> dict:
    key = jax.random.key(seed)
    ks = iter(jax.random.split(key, 32))
    f32 = jnp.float32

    def normal(shape, fan_in):
        return jax.random.normal(next(ks), shape, f32) * (fan_in ** -0.5)

    def gain(shape):
        return 1.0 + 0.02 * jax.random.normal(next(ks), shape, f32)

    x = jax.random.normal(next(ks), (BATCH, SEQ, D_MODEL), f32)
    p = jax.random.normal(next(ks), (DEPTH, BATCH, SEQ, PLE_DIM), f32)

    norm_mix_pre = gain((DEPTH, D_MODEL))
    norm_mix_post = gain((DEPTH, D_MODEL))
    norm_ffn_pre = gain((DEPTH, D_MODEL))
    norm_ffn_post = gain((DEPTH, D_MODEL))

    ssd_w_in = normal((N_SSD_LAYERS, D_MODEL, SSD_IN_DIM), D_MODEL)
    ssd_conv_w = normal((N_SSD_LAYERS, SSD_CONV_K, SSD_CONV_DIM), SSD_CONV_K)
    ssd_conv_b = 0.02 * jax.random.normal(next(ks), (N_SSD_LAYERS, SSD_CONV_DIM), f32)
    u = jax.random.uniform(next(ks), (N_SSD_LAYERS, SSD_N_HEADS), f32)
    dt0 = jnp.exp(u * (math.log(0.1) - math.log(0.001)) + math.log(0.001))
    ssd_dt_bias = dt0 + jnp.log(-jnp.expm1(-dt0))
    ssd_a_log = jnp.log(jax.random.uniform(next(ks), (N_SSD_LAYERS, SSD_N_HEADS), f32, 1.0, 16.0))
    ssd_d = gain((N_SSD_LAYERS, SSD_N_HEADS))
    ssd_norm_w = gain((N_SSD_LAYERS, SSD_D_INNER))
    ssd_w_out = normal((N_SSD_LAYERS, SSD_D_INNER, D_MODEL), SSD_D_INNER)

    fox_w_in = normal((N_FOX_LAYERS, D_MODEL, FOX_IN_DIM), D_MODEL)
    fox_b_f = jax.random.uniform(next(ks), (N_FOX_LAYERS, FOX_N_HEADS), f32, 1.0, 4.0)
    fox_w_out = normal((N_FOX_LAYERS, FOX_WIDTH, D_MODEL), FOX_WIDTH)

    ffn_w_gate = normal((DEPTH, D_MODEL, FFN_HIDDEN), D_MODEL)
    ffn_w_up = normal((DEPTH, D_MODEL, FFN_HIDDEN), D_MODEL)
    ffn_w_down = normal((DEPTH, FFN_HIDDEN, D_MODEL), FFN_HIDDEN)

    ple_w_proj = normal((DEPTH, PLE_DIM, D_MODEL), PLE_DIM)
    ple_norm = gain((DEPTH, D_MODEL))
    ple_w_gate = normal((DEPTH, D_MODEL, D_MODEL), D_MODEL)

    return {
        "x": x, "p": p,
        "norm_mix_pre": norm_mix_pre, "norm_mix_post": norm_mix_post,
        "norm_ffn_pre": norm_ffn_pre, "norm_ffn_post": norm_ffn_post,
        "ssd_w_in": ssd_w_in, "ssd_conv_w": ssd_conv_w, "ssd_conv_b": ssd_conv_b,
        "ssd_dt_bias": ssd_dt_bias, "ssd_a_log": ssd_a_log, "ssd_d": ssd_d,
        "ssd_norm_w": ssd_norm_w, "ssd_w_out": ssd_w_out,
        "fox_w_in": fox_w_in, "fox_b_f": fox_b_f, "fox_w_out": fox_w_out,
        "ffn_w_gate": ffn_w_gate, "ffn_w_up": ffn_w_up, "ffn_w_down": ffn_w_down,
        "ple_w_proj": ple_w_proj, "ple_norm": ple_norm, "ple_w_gate": ple_w_gate,
    }


def reference(x, p, norm_mix_pre, norm_mix_post, norm_ffn_pre, norm_ffn_post,
              ssd_w_in, ssd_conv_w, ssd_conv_b, ssd_dt_bias, ssd_a_log, ssd_d,
              ssd_norm_w, ssd_w_out, fox_w_in, fox_b_f, fox_w_out,
              ffn_w_gate, ffn_w_up, ffn_w_down, ple_w_proj, ple_norm, ple_w_gate):
    h = x
    for i in range(DEPTH):
        j = i // N_MIXERS
        hn = rmsnorm(h, norm_mix_pre[i])
        if i % N_MIXERS == 0:
            mix = ssd_mixer(hn, ssd_w_in[j], ssd_conv_w[j], ssd_conv_b[j], ssd_dt_bias[j],
                            ssd_a_log[j], ssd_d[j], ssd_norm_w[j], ssd_w_out[j])
        else:
            mix = fox_mixer(hn, fox_w_in[j], fox_b_f[j], fox_w_out[j])
        h = h + rmsnorm(mix, norm_mix_post[i])
        ff = swiglu(rmsnorm(h, norm_ffn_pre[i]), ffn_w_gate[i], ffn_w_up[i], ffn_w_down[i])
        h = h + rmsnorm(ff, norm_ffn_post[i])
        pe = rmsnorm(p[i] @ ple_w_proj[i], ple_norm[i])
        h = h + jax.nn.sigmoid(h @ ple_w_gate[i]) * pe
    return h
```

```python
import numpy as np
from contextlib import ExitStack
import concourse.bass as bass
import concourse.mybir as mybir
from concourse.bass_utils import run_bass_kernel_spmd

F32 = mybir.dt.float32
BF16 = mybir.dt.bfloat16
AF = mybir.ActivationFunctionType
ALU = mybir.AluOpType

D = 2048
SEQ = 2048
BATCH = 4
DI = 4096
NH = 64
HP = 64
NG = 8
DS = 128
CONV_DIM = 6144
SSD_IN = 10304
FH = 16
FD = 128
FOX_IN = 6160
FFN = 5632
PLE = 256
EPS = 1e-6
DEFER = 2
SG = 1024
ALAG = 2
NGRP = SG // 512
NT = SG // 128


_P = [None]


class Buf:
    __slots__ = ("name", "w", "wd", "r", "rd", "excl")

    def __init__(self, name="", excl=False):
        self.name = name
        self.w = {}
        self.wd = []
        self.r = {}
        self.rd = []
        self.excl = excl
        p = _P[0]
        if p is not None and p.front_c is not None:
            self.r = dict(p.front_c)
            self.rd = list(p.front_d)


class Op:
    __slots__ = ("eng", "fn", "deps", "dma", "needed", "sig", "cc")

    def __init__(self, eng, fn, dma):
        self.eng = eng
        self.fn = fn
        self.dma = dma
        self.deps = []
        self.needed = False
        self.sig = None
        self.cc = False


ENGS = ("pe", "act", "dve", "pool", "sp")
DMA_K = 8


class Prog:
    def __init__(self, nc):
        self.nc = nc
        self.by_eng = {e: [] for e in ENGS}
        self.dma_hist = {e: [] for e in ENGS}
        self.nops = 0
        self.front_c = None
        self.front_d = None
        _P[0] = self

    def mark(self):
        fc = {}
        fd = []
        for e in ENGS:
            for o in reversed(self.by_eng[e]):
                if not o.dma:
                    fc[e] = o
                    break
            fd.extend(self.dma_hist[e][-DMA_K:])
        self.front_c = fc
        self.front_d = fd

    def op(self, eng, fn, reads=(), writes=(), dma=False, extra_deps=(), cc=False):
        o = Op(eng, fn, dma or cc)
        o.cc = cc
        is_cc = cc
        dma = dma or cc
        deps = {}

        def add(d, raw):
            if d is o:
                return
            if (not d.dma) and (not dma) and d.eng == eng:
                if eng == "pe" or not raw:
                    return
            deps[id(d)] = d

        for b in reads:
            for d in b.w.values():
                add(d, True)
            for d in b.wd:
                add(d, True)
            if b.excl:
                for d in b.r.values():
                    add(d, False)
                for d in b.rd:
                    add(d, False)
        for b in writes:
            for d in b.w.values():
                add(d, False)
            for d in b.wd:
                add(d, False)
            for d in b.r.values():
                add(d, False)
            for d in b.rd:
                add(d, False)
        for d in extra_deps:
            if d is not None:
                deps[id(d)] = d
        if dma and not is_cc:
            h = self.dma_hist[eng]
            if len(h) >= DMA_K:
                d = h[len(h) - DMA_K]
                deps[id(d)] = d
            h.append(o)
        o.deps = list(deps.values())
        for d in o.deps:
            if not d.dma:
                d.needed = True
        for b in reads:
            if b.excl:
                b.w = {}
                b.wd = []
                b.r = {}
                b.rd = []
                if dma:
                    b.wd.append(o)
                else:
                    b.w[eng] = o
            else:
                if dma:
                    b.rd.append(o)
                else:
                    b.r[eng] = o
        for b in writes:
            if b.r or b.rd:
                b.w = {}
                b.wd = []
                b.r = {}
                b.rd = []
            if dma:
                b.wd.append(o)
            else:
                b.w[eng] = o
        self.by_eng[eng].append(o)
        self.nops += 1
        return o

    def emit(self, stack):
        nc = self.nc
        csem = {e: stack.enter_context(nc.semaphore("c_" + e)) for e in ("pe", "act", "dve", "pool")}
        dsem = {e: [stack.enter_context(nc.semaphore("d_%s%d" % (e, i))) for i in range(DMA_K)]
                for e in ENGS if self.dma_hist[e]}
        for e in ENGS:
            cnt = 0
            nd = 0
            for o in self.by_eng[e]:
                if o.cc:
                    o.sig = (stack.enter_context(nc.semaphore("cc%d" % id(o))), 1)
                elif o.dma:
                    o.sig = (dsem[e][nd % DMA_K], 16 * (nd // DMA_K + 1))
                    nd += 1
                elif o.needed:
                    cnt += 1
                    o.sig = (csem[e], cnt)
        block = stack.enter_context(nc.Block())
        hook = {"pe": block.tensor, "act": block.scalar, "dve": block.vector,
                "pool": block.gpsimd, "sp": block.sync}

        def make(e):
            def body(eng):
                waited = {}
                for o in self.by_eng[e]:
                    for d in o.deps:
                        if d.sig is None:
                            continue
                        s, v = d.sig
                        k = id(s)
                        if waited.get(k, 0) < v:
                            eng.wait_ge(s, v)
                            waited[k] = v
                    ins = o.fn(eng)
                    if ins is None:
                        continue
                    if o.cc:
                        ins.then_inc(o.sig[0])
                    elif o.dma:
                        ins.then_inc(o.sig[0], 16)
                    elif o.sig is not None:
                        ins.then_inc(o.sig[0], 1)
            return body

        for e in ENGS:
            if self.by_eng[e]:
                hook[e](make(e))


class Rot:
    def __init__(self, K, name, shape, dtype, n):
        self.t = [K.sb(name + str(i), shape, dtype) for i in range(n)]
        self.b = [Buf(name + str(i)) for i in range(n)]
        self.i = 0

    def next(self):
        i = self.i % len(self.t)
        self.i += 1
        return self.t[i], self.b[i]


class Kern:
    pass


def build_program(nsg, debug=(), paired=False):
    TC = nsg * SG
    nc = bass.Bass("TRN2", target_bir_lowering=False)
    K = Kern()
    K.nc = nc
    st = ExitStack()
    K.st = st
    P = Prog(nc)
    K.P = P

    def ein(name, shape, dt=F32):
        return nc.dram_tensor(name, list(shape), dt, kind="ExternalInput").ap()

    def scratch(name, shape, dt):
        kind = "ExternalOutput" if name in debug else "Internal"
        return nc.dram_tensor(name, list(shape), dt, kind=kind).ap()

    x_in = ein("x", [TC, D])
    if paired:
        xp_in = ein("xp", [SG, D])
        role_in = ein("role", [128, 2])
    p_in = ein("p", [2, TC, PLE])
    vec_fm = ein("vec_fm", [128, 10, 16])
    ssd_nw = ein("ssd_nw", [128, 32])
    conv_w = ein("conv_w", [128, 48, 4])
    conv_b = ein("conv_b", [128, 48])
    rep64 = ein("rep64", [128, 3, 64])
    rep16 = ein("rep16", [128, 16])
    ssd_w_in = ein("ssd_w_in", [D, SSD_IN])
    ssd_w_out = ein("ssd_w_out", [DI, D])
    fox_w_in = ein("fox_w_in", [D, FOX_IN])
    fox_w_out = ein("fox_w_out", [D, D])
    ffn_wg = ein("ffn_w_gate", [2, D, FFN])
    ffn_wu = ein("ffn_w_up", [2, D, FFN])
    ffn_wd = ein("ffn_w_down", [2, FFN, D])
    ple_wp = ein("ple_w_proj", [2, PLE, D])
    ple_wg = ein("ple_w_gate", [2, D, D])
    out = nc.dram_tensor("out", [TC, D], F32, kind="ExternalOutput").ap()

    D_hT = scratch("D_hT", [16, 128, TC], F32)
    D_mix = scratch("D_mix", [16, 128, SG], F32)
    D_g = scratch("D_g", [16, 128, SG], F32)
    D_sz = scratch("D_sz", [32, 128, SG], BF16)
    D_xs = scratch("D_xs", [NT, 128, DI], BF16)
    D_BT = scratch("D_BT", [NG, 128, SG], BF16)
    D_CT = scratch("D_CT", [NG, 128, SG], BF16)
    D_Btm = scratch("D_Btm", [NT, 128, NG * DS], BF16)
    D_dtT = scratch("D_dtT", [64, SG], F32)
    D_yn = scratch("D_yn", [32, 128, SG], BF16)
    D_QT = scratch("D_QT", [FH, 128, SG], BF16)
    D_KT = scratch("D_KT", [FH, 128, TC], BF16)
    D_V = scratch("D_V", [TC // 128, 128, D], BF16)
    D_oT = scratch("D_oT", [FH, 128, SG], BF16)
    D_prev = scratch("D_prev", [128, NG * 512], F32)
    D_xbT = scratch("D_xbT", [40, 128, SG], BF16)
    B_xbT = Buf("xbT")
    if paired:
        D_hTp = scratch("D_hTp", [16, 128, SG], F32)
        B_hTp = [Buf("hTp%d" % k) for k in range(16)]
        X_K = [scratch("X_K%d" % j, [512, SG], BF16) for j in range(4)]
        G_K = [scratch("G_K%d" % j, [1024, SG], BF16) for j in range(4)]
        X_V = [scratch("X_V%d" % j, [256, D], BF16) for j in range(4)]
        G_V = [scratch("G_V%d" % j, [512, D], BF16) for j in range(4)]
        X_S = scratch("X_S", [128, 128], F32)
        G_S = scratch("G_S", [256, 128], F32)
        B_XK = [Buf("XK%d" % j) for j in range(4)]
        B_GK = [Buf("GK%d" % j) for j in range(4)]
        B_XV = [Buf("XV%d" % j) for j in range(4)]
        B_GV = [Buf("GV%d" % j) for j in range(4)]
        B_XS = Buf("XS")
        B_GS = Buf("GS")
        PAIRS = [[0, 1], [2, 3], [4, 5], [6, 7]]
    B_hT = [Buf("hT%d" % k) for k in range(16)]
    B_mix = [Buf("mix%d" % k) for k in range(16)]
    B_g = [Buf("g%d" % k) for k in range(16)]
    B_sz, B_xs, B_BT, B_CT, B_Btm, B_dtT, B_yn = (Buf(n) for n in ("sz", "xs", "BT", "CT", "Btm", "dtT", "yn"))
    B_QT, B_KT, B_V, B_oT, B_out = (Buf(n) for n in ("QT", "KT", "V", "oT", "out"))

    uniq = [0]

    def sbt(stk, name, shape, dt):
        uniq[0] += 1
        return stk.enter_context(nc.sbuf_tensor("%s_%d" % (name, uniq[0]), list(shape), dt))

    def sb(name, shape, dt):
        return sbt(st, name, shape, dt)

    class scope:
        def __enter__(self):
            self.stk = ExitStack()
            return self.stk

        def __exit__(self, *a):
            self.stk.close()
            P.mark()
            return False

    K.sb = sb

    ps = [st.enter_context(nc.psum_tensor("ps%d" % i, [128, 512], F32)) for i in range(8)]
    psb = [Buf("bank%d" % i, excl=True) for i in range(8)]
    bank_rr = [0]

    def bank(lo=0, hi=8):
        n = hi - lo
        i = lo + bank_rr[0] % n
        bank_rr[0] += 1
        return ps[i], psb[i]

    identf = sb("identf", [128, 128], F32)
    ident = sb("ident", [128, 128], BF16)
    onesf = sb("onesf", [128, 128], F32)
    onesb = sb("onesb", [128, 128], BF16)
    triuf = sb("triuf", [128, 128], F32)
    triub = sb("triub", [128, 128], BF16)
    negm = sb("negm", [128, 128], BF16)
    negf = sb("negf", [128, 128], F32)
    selS = sb("selS", [64, 64, 128], BF16)
    selF = sb("selF", [16, 16, 128], BF16)
    vecs = sb("vecs", [128, 10, 16], F32)
    nw_s = sb("nw_s", [128, 32], F32)
    cw_s = sb("cw_s", [128, 48, 4], F32)
    cb_s = sb("cb_s", [128, 48], F32)
    r64 = sb("r64", [128, 3, 64], F32)
    r16 = sb("r16", [128, 16], F32)
    arep = sb("arep", [128, 64], F32)
    halo = sb("halo", [128, 48, 3], F32)
    carryb = sb("carryb", [128, 16], F32)
    ncs_all = sb("ncs_all", [128, (2 * SG if paired else TC) // 128, 16], F32)
    role_s = sb("role_s", [128, 2], F32)
    csThi = sb("csThi", [16, SG], BF16)
    csTlo = sb("csTlo", [16, SG], BF16)
    Bc = Buf("consts")
    B_halo = Buf("halo")
    B_Dprev = Buf("Dprev")
    B_carry = Buf("carry")
    B_ncs = Buf("ncs")
    B_csT = Buf("csT")

    def pool_c(fn, w=(Bc,), r=()):
        return P.op("pool", fn, reads=r, writes=w)

    pool_c(lambda e: e.memset(identf[:], 0.0))
    pool_c(lambda e: e.affine_select(out=identf[:], in_=identf[:], pattern=[[-1, 128]],
                                     compare_op=ALU.not_equal, fill=1.0, base=0, channel_multiplier=1))
    pool_c(lambda e: e.memset(onesf[:], 1.0))
    pool_c(lambda e: e.memset(triuf[:], 1.0))
    pool_c(lambda e: e.affine_select(out=triuf[:], in_=triuf[:], pattern=[[1, 128]],
                                     compare_op=ALU.is_ge, fill=0.0, base=0, channel_multiplier=-1))
    pool_c(lambda e: e.memset(negf[:], 0.0))
    pool_c(lambda e: e.affine_select(out=negf[:], in_=negf[:], pattern=[[1, 128]],
                                     compare_op=ALU.is_ge, fill=-30000.0, base=0, channel_multiplier=-1))
    pool_c(lambda e: e.memset(selS[:], 0.0))
    pool_c(lambda e: e.affine_select(out=selS[:], in_=selS[:], pattern=[[-1, 64], [0, 128]],
                                     compare_op=ALU.not_equal, fill=1.0, base=0, channel_multiplier=1))
    pool_c(lambda e: e.memset(halo[:], 0.0), w=(B_halo,))
    pool_c(lambda e: e.memset(carryb[:], 0.0), w=(B_carry,))
    P.op("dve", lambda e: e.tensor_copy(out=ident[:], in_=identf[:]), reads=[Bc], writes=[Bc])
    P.op("dve", lambda e: e.tensor_copy(out=onesb[:], in_=onesf[:]), reads=[Bc], writes=[Bc])
    P.op("dve", lambda e: e.tensor_copy(out=triub[:], in_=triuf[:]), reads=[Bc], writes=[Bc])
    P.op("dve", lambda e: e.tensor_copy(out=negm[:], in_=negf[:]), reads=[Bc], writes=[Bc])
    P.op("dve", lambda e: e.tensor_copy(out=selF[:], in_=selS[0:16, 0:16, :]), reads=[Bc], writes=[Bc])
    if paired:
        P.op("sp", lambda e: e.dma_start(out=role_s[:], in_=role_in), writes=[Bc], dma=True)
    for dst, src in ((vecs, vec_fm), (nw_s, ssd_nw), (cw_s, conv_w), (cb_s, conv_b), (r64, rep64), (r16, rep16)):
        P.op("sp", lambda e, dst=dst, src=src: e.dma_start(out=dst[:], in_=src), writes=[Bc], dma=True)
    P.op("act", lambda e: e.activation(out=arep[:], in_=r64[:, 1, :], func=AF.Exp), reads=[Bc], writes=[Bc])
    P.op("dve", lambda e: e.tensor_scalar(out=arep[:], in0=arep[:], scalar1=-1.0, scalar2=None, op0=ALU.mult),
         reads=[Bc], writes=[Bc])

    def dma(q, out_ap, in_ap, reads, writes):
        return P.op(q, lambda e: e.dma_start(out=out_ap, in_=in_ap), reads=reads, writes=writes, dma=True)

    def rstd_from_ssq(ss, bss, dst, bdst, cols, inv_n, tmp_rot):
        tmp, btmp = tmp_rot.next()
        n = ss.shape[1]
        P.op("act", lambda e: e.activation(out=tmp[:, 0:n], in_=ss, func=AF.Sqrt, bias=epsb[:, 0:1], scale=inv_n),
             reads=[bss, Bc], writes=[btmp])
        P.op("dve", lambda e: e.reciprocal(out=dst[:, cols], in_=tmp[:, 0:n]), reads=[btmp], writes=[bdst])

    epsb = sb("epsb", [128, 1], F32)
    pool_c(lambda e: e.memset(epsb[:], EPS))

    def norm_fm(stk, c0, vidx, name, nhb=2, Dsrc=None, Bsrc=None):
        hnT = sbt(stk, name + "_hnT", [128, 16, SG], BF16)
        b_hn = Buf(name + "_hnT")
        if Dsrc is None:
            Dsrc, Bsrc = D_hT, B_hT
        with scope() as loc:
            hb = [sbt(loc, name + "_hb%d" % i, [128, 16, 512], F32) for i in range(nhb)]
            bhb = [Buf("hb%d" % i) for i in range(nhb)]
            sq = [sbt(loc, name + "_sq%d" % i, [128, 512], BF16) for i in range(3)]
            bsq = [Buf("sq%d" % i) for i in range(3)]
            rs = sbt(loc, name + "_rs", [128, SG], F32)
            brs = Buf("rs")
            tr = Rot(K2(loc), name + "_tmp", [128, 512], F32, 2)
            for g in range(NGRP):
                cs = slice(c0 + g * 512, c0 + (g + 1) * 512)
                for k in range(16):
                    dma("sp", hb[g % nhb][:, k, :], Dsrc[k, :, cs], [Bsrc[k]], [bhb[g % nhb]])
                pt, pb = bank()
                for k in range(16):
                    i = (g * 16 + k) % 3
                    P.op("act", lambda e, i=i, k=k, g=g: e.activation(out=sq[i][:], in_=hb[g % nhb][:, k, :], func=AF.Square),
                         reads=[bhb[g % nhb]], writes=[bsq[i]])
                    P.op("pe", lambda e, i=i, k=k, pt=pt: e.matmul(pt[:, :], lhsT=onesb[:], rhs=sq[i][:], start=(k == 0), stop=(k == 15)),
                         reads=[bsq[i], Bc], writes=[pb])
                rstd_from_ssq(pt[:, :], pb, rs, brs, slice(g * 512, (g + 1) * 512), 1.0 / D, tr)
                for k in range(16):
                    P.op("dve", lambda e, k=k, g=g: e.scalar_tensor_tensor(
                        out=hnT[:, k, g * 512:(g + 1) * 512], in0=hb[g % nhb][:, k, :], scalar=vecs[:, vidx, k:k + 1],
                        in1=rs[:, g * 512:(g + 1) * 512], op0=ALU.mult, op1=ALU.mult),
                        reads=[bhb[g % nhb], brs, Bc], writes=[b_hn])
        return hnT, b_hn

    class K2:
        def __init__(self, stk):
            self.stk = stk

        def sb(self, name, shape, dt):
            return sbt(self.stk, name, shape, dt)

    def linear_fm(stk, name, Ws, KC, col0, ncols, xT, b_x, consumer, CW, nbuf=3, lo=0, hi=6):
        nW = len(Ws)
        wb = [sbt(stk, "%s_w%d" % (name, i), [128, KC, CW], BF16) for i in range(nbuf * nW)]
        bw = [Buf("%s_w%d" % (name, i)) for i in range(nbuf * nW)]
        nblk = (ncols + CW - 1) // CW
        KS = 8 if KC % 8 == 0 else (11 if KC % 11 == 0 else KC)
        pend = []
        for b in range(nblk):
            c0 = col0 + b * CW
            cw = min(CW, ncols - b * CW)
            slots = []
            for wi, W in enumerate(Ws):
                s = (b % nbuf) * nW + wi
                slots.append(s)
                for k0 in range(0, KC, KS):
                    dma("pool", wb[s][:, k0:k0 + KS, 0:cw],
                        W[k0 * 128:(k0 + KS) * 128, c0:c0 + cw].rearrange("(k p) c -> p k c", p=128), [], [bw[s]])
            for cc in range(0, cw, 128):
                rows = min(128, cw - cc)
                co = (col0 + b * CW + cc) // 128
                for g in range(NGRP):
                    pts = []
                    pbs = []
                    for wi in range(nW):
                        pt, pb = bank(lo, hi)
                        pts.append(pt)
                        pbs.append(pb)
                        s = slots[wi]
                        for k in range(KC):
                            P.op("pe", lambda e, s=s, k=k, cc=cc, rows=rows, pt=pt, g=g: e.matmul(
                                pt[0:rows, :], lhsT=wb[s][:, k, cc:cc + rows], rhs=xT[:, k, g * 512:(g + 1) * 512],
                                start=(k == 0), stop=(k == KC - 1)), reads=[bw[s], b_x], writes=[pb])
                    late = consumer(co, g, pts, pbs, rows)
                    if late is not None:
                        pend.append(late)
                    while len(pend) > DEFER:
                        pend.pop(0)()
        for f_ in pend:
            f_()

    def post_consumer(stk, name, ssb):
        sq = Rot(K2(stk), name + "_psq", [128, 512], BF16, 5)
        mb = Rot(K2(stk), name + "_pmb", [128, 512], F32, 3)

        def cons(co, g, pts, pbs, rows, nco=16):
            pt, pb = pts[0], pbs[0]
            s, bs = sq.next()
            P.op("act", lambda e: e.activation(out=s[:], in_=pt[:, :], func=AF.Square), reads=[pb], writes=[bs])
            m, bm = mb.next()
            P.op("dve", lambda e: e.tensor_copy(out=m[:], in_=pt[:, :]), reads=[pb], writes=[bm])
            dma("sp", D_mix[co, :, g * 512:(g + 1) * 512], m[:], [bm], [B_mix[co]])

            def late():
                P.op("pe", lambda e: e.matmul(ps[6 + g][:, :], lhsT=onesb[:], rhs=s[:], start=(co == 0), stop=(co == nco - 1)),
                     reads=[bs, Bc], writes=[psb[6 + g]])
            return late
        return cons

    def post_finalize(stk, name, c0, vidx, gated):
        rs = sbt(stk, name + "_prs", [128, SG], F32)
        brs = Buf("prs")
        tr = Rot(K2(stk), name + "_ptmp", [128, 512], F32, 2)
        for g in range(NGRP):
            rstd_from_ssq(ps[6 + g][:, :], psb[6 + g], rs, brs, slice(g * 512, (g + 1) * 512), 1.0 / D, tr)
        mr = Rot(K2(stk), name + "_fm", [128, 512], F32, 4)
        hr = Rot(K2(stk), name + "_fh", [128, 512], F32, 4)
        gr = Rot(K2(stk), name + "_fg", [128, 512], F32, 4) if gated else None
        blocks = [(k, g) for k in range(16) for g in range(NGRP)]
        loaded = {}

        def load(n):
            k, g = blocks[n]
            cs = slice(g * 512, (g + 1) * 512)
            hs = slice(c0 + g * 512, c0 + (g + 1) * 512)
            m, bm = mr.next()
            h, bh = hr.next()
            dma("sp", m[:], D_mix[k, :, cs], [B_mix[k]], [bm])
            dma("sp", h[:], D_hT[k, :, hs], [B_hT[k]], [bh])
            gt = bg = None
            if gated:
                gt, bg = gr.next()
                dma("sp", gt[:], D_g[k, :, cs], [B_g[k]], [bg])
            loaded[n] = (m, bm, h, bh, gt, bg)

        PF = 2
        for n in range(min(PF, len(blocks))):
            load(n)
        for n in range(len(blocks)):
            if n + PF < len(blocks):
                load(n + PF)
            k, g = blocks[n]
            cs = slice(g * 512, (g + 1) * 512)
            hs = slice(c0 + g * 512, c0 + (g + 1) * 512)
            m, bm, h, bh, gt, bg = loaded.pop(n)
            P.op("dve", lambda e, m=m, k=k, cs=cs: e.scalar_tensor_tensor(
                out=m[:], in0=m[:], scalar=vecs[:, vidx, k:k + 1], in1=rs[:, cs], op0=ALU.mult, op1=ALU.mult),
                reads=[bm, brs, Bc], writes=[bm])
            if gated:
                P.op("dve", lambda e, m=m, gt=gt: e.tensor_tensor(out=m[:], in0=m[:], in1=gt[:], op=ALU.mult),
                     reads=[bm, bg], writes=[bm])
            P.op("dve", lambda e, m=m, h=h: e.tensor_tensor(out=h[:], in0=h[:], in1=m[:], op=ALU.add),
                 reads=[bm, bh], writes=[bh])
            dma("sp", D_hT[k, :, hs], h[:], [bh], [B_hT[k]])

    def load_xT(stk, name, Dsrc, bsrc, KC, c0=0):
        xT = sbt(stk, name, [128, KC, SG], BF16)
        bx = Buf(name)
        for k0 in range(0, KC, 8):
            dma("sp", xT[:, k0:k0 + 8, :], Dsrc[k0:k0 + 8, :, c0:c0 + SG].rearrange("k p t -> p k t"), [bsrc], [bx])
        return xT, bx

    def phase_input(s, src=None, Ddst=None, Bdst=None):
        c0 = s * SG
        if src is None:
            src, Ddst, Bdst = x_in, D_hT, B_hT
        with scope() as loc:
            xr = Rot(K2(loc), "in_x", [128, D], F32, 3)
            tr = Rot(K2(loc), "in_t", [128, 16, 128], F32, 2)
            pre = {}

            def load(i):
                xt, bx = xr.next()
                dma("sp", xt[:], src[c0 + i * 128:c0 + (i + 1) * 128, :], [], [bx])
                pre[i] = (xt, bx)

            load(0)
            for i in range(NT):
                if i + 1 < NT:
                    load(i + 1)
                xt, bx = pre.pop(i)
                tt, bt = tr.next()
                for q in range(4):
                    pt, pb = bank()
                    for j in range(4):
                        k = q * 4 + j
                        P.op("pe", lambda e, k=k, j=j, pt=pt, xt=xt: e.transpose(
                            out=pt[:, j * 128:(j + 1) * 128], in_=xt[:, k * 128:(k + 1) * 128], identity=identf[:]),
                            reads=[bx, Bc], writes=[pb])
                    P.op("act" if q % 2 else "dve",
                         (lambda e, q=q, pt=pt, tt=tt: e.copy(out=tt[:, q * 4:(q + 1) * 4, :].rearrange("p a b -> p (a b)"), in_=pt[:, :])) if q % 2 else
                         (lambda e, q=q, pt=pt, tt=tt: e.tensor_copy(out=tt[:, q * 4:(q + 1) * 4, :].rearrange("p a b -> p (a b)"), in_=pt[:, :])),
                         reads=[pb], writes=[bt])
                dma("sp", Ddst[:, :, c0 + i * 128:c0 + (i + 1) * 128].rearrange("k p t -> p k t"), tt[:], [bt], Bdst)

    def phase_output(s):
        c0 = s * SG
        with scope() as loc:
            hr = Rot(K2(loc), "o_h", [128, 16, 128], F32, 3)
            orr = Rot(K2(loc), "o_o", [128, D], F32, 2)
            pre = {}

            def load(i):
                ht, bh = hr.next()
                dma("sp", ht[:], D_hT[:, :, c0 + i * 128:c0 + (i + 1) * 128].rearrange("k p t -> p k t"), B_hT, [bh])
                pre[i] = (ht, bh)

            load(0)
            for i in range(NT):
                if i + 1 < NT:
                    load(i + 1)
                ht, bh = pre.pop(i)
                ot, bo = orr.next()
                for q in range(4):
                    pt, pb = bank()
                    for j in range(4):
                        k = q * 4 + j
                        P.op("pe", lambda e, k=k, j=j, pt=pt, ht=ht: e.transpose(
                            out=pt[:, j * 128:(j + 1) * 128], in_=ht[:, k, :], identity=identf[:]),
                            reads=[bh, Bc], writes=[pb])
                    P.op("act" if q % 2 else "dve",
                         (lambda e, q=q, pt=pt, ot=ot: e.copy(out=ot[:, q * 512:(q + 1) * 512], in_=pt[:, :])) if q % 2 else
                         (lambda e, q=q, pt=pt, ot=ot: e.tensor_copy(out=ot[:, q * 512:(q + 1) * 512], in_=pt[:, :])),
                         reads=[pb], writes=[bo])
                dma("sp", out[c0 + i * 128:c0 + (i + 1) * 128, :], ot[:], [bo], [B_out])

    def phase_ssd_inproj(s, prefix=False):
        c0 = s * SG
        with scope() as loc:
            if prefix:
                hnT, b_hn = norm_fm(loc, 0, 0, "n0p", Dsrc=D_hTp, Bsrc=B_hTp)
            else:
                hnT, b_hn = norm_fm(loc, c0, 0, "n0")
            szr = Rot(K2(loc), "b0_sz", [128, 512], BF16, 3)
            xpr = Rot(K2(loc), "b0_xp", [128, 515], F32, 3)
            acr = Rot(K2(loc), "b0_ac", [128, 512], F32, 3)
            xbr = Rot(K2(loc), "b0_xb", [128, 512], BF16, 5)
            str_ = Rot(K2(loc), "b0_st", [128, 4, 128], BF16, 3)
            dtr = Rot(K2(loc), "b0_dt", [64, 512], F32, 2)

            def cons(co, g, pts, pbs, rows):
                pt, pb = pts[0], pbs[0]
                cs = slice(g * 512, (g + 1) * 512)
                if co < 32:
                    t, bt = szr.next()
                    P.op("act", lambda e: e.activation(out=t[:], in_=pt[:, :], func=AF.Silu), reads=[pb], writes=[bt])
                    dma("sp", D_sz[co, :, cs], t[:], [bt], [B_sz])
                elif co < 80:
                    cc = co - 32
                    xp, bxp = xpr.next()
                    P.op("act", lambda e: e.copy(out=xp[:, 0:3], in_=halo[:, cc, :]), reads=[B_halo], writes=[bxp])
                    P.op("act", lambda e: e.copy(out=xp[:, 3:515], in_=pt[:, :]), reads=[pb], writes=[bxp])
                    P.op("dve", lambda e: e.tensor_copy(out=halo[:, cc, :], in_=xp[:, 512:515]), reads=[bxp], writes=[B_halo])
                    if prefix and cc >= 40:
                        return
                    ac, bac = acr.next()
                    P.op("dve", lambda e: e.tensor_scalar(out=ac[:], in0=xp[:, 0:512], scalar1=cw_s[:, cc, 0:1],
                                                          scalar2=cb_s[:, cc:cc + 1], op0=ALU.mult, op1=ALU.add),
                         reads=[bxp, Bc], writes=[bac])
                    for j in (1, 2, 3):
                        P.op("dve", lambda e, j=j: e.scalar_tensor_tensor(out=ac[:], in0=xp[:, j:j + 512], scalar=cw_s[:, cc, j:j + 1],
                                                                          in1=ac[:], op0=ALU.mult, op1=ALU.add),
                             reads=[bxp, bac, Bc], writes=[bac])
                    xb, bxb = xbr.next()
                    P.op("act", lambda e: e.activation(out=xb[:], in_=ac[:], func=AF.Silu), reads=[bac], writes=[bxb])
                    late = None
                    if cc < 40:
                        dma("sp", D_xbT[cc, :, cs], xb[:], [bxb], [B_xbT])
                    if 32 <= cc < 40 and not prefix:
                        dma("sp", D_BT[cc - 32, :, cs], xb[:], [bxb], [B_BT])
                    elif cc >= 40:
                        dma("sp", D_CT[cc - 40, :, cs], xb[:], [bxb], [B_CT])
                    return late
                else:
                    t, bt = dtr.next()
                    P.op("act", lambda e: e.copy(out=t[:], in_=pt[0:64, :]), reads=[pb], writes=[bt])
                    dma("sp", D_dtT[:, cs], t[:], [bt], [B_dtT])

            if prefix:
                linear_fm(loc, "b0p", [ssd_w_in], 16, 4096, 6144, hnT, b_hn, cons, CW=512)
                linear_fm(loc, "b0q", [ssd_w_in], 16, 10240, 64, hnT, b_hn, cons, CW=64, nbuf=1)
                P.op("dve", lambda e: e.tensor_scalar(out=halo[:], in0=halo[:], scalar1=role_s[:, 0:1], scalar2=None, op0=ALU.mult),
                     reads=[B_halo, Bc], writes=[B_halo])
            else:
                linear_fm(loc, "b0", [ssd_w_in], 16, 0, SSD_IN, hnT, b_hn, cons, CW=512)
            xinr = Rot(K2(loc), "b0_xin", [128, 40, 128], BF16, 2)
            stgr = Rot(K2(loc), "b0_stg", [128, 40 * 128], BF16, 2)
            pre = {}

            def tload(i):
                xin, bxin = xinr.next()
                for c8 in range(0, 40, 8):
                    dma("sp", xin[:, c8:c8 + 8, :], D_xbT[c8:c8 + 8, :, i * 128:(i + 1) * 128].rearrange("c p t -> p c t"), [B_xbT], [bxin])
                pre[i] = (xin, bxin)

            tload(0)
            for i in range(NT):
                if i + 1 < NT:
                    tload(i + 1)
                xin, bxin = pre.pop(i)
                stg, bstg = stgr.next()
                for q in range(10):
                    pt2, pb2 = bank(0, 6)
                    pv = pt2[:, :].bitcast(BF16)
                    for j in range(4):
                        P.op("pe", lambda e, pv=pv, j=j, q=q, xin=xin: e.transpose(out=pv[:, j * 128:(j + 1) * 128], in_=xin[:, q * 4 + j, :], identity=ident[:]),
                             reads=[bxin, Bc], writes=[pb2])
                    if q % 2:
                        P.op("act", lambda e, pv=pv, q=q, stg=stg: e.copy(out=stg[:, q * 512:(q + 1) * 512], in_=pv[:, 0:512]), reads=[pb2], writes=[bstg])
                    else:
                        P.op("dve", lambda e, pv=pv, q=q, stg=stg: e.tensor_copy(out=stg[:, q * 512:(q + 1) * 512], in_=pv[:, 0:512]), reads=[pb2], writes=[bstg])
                dma("sp", D_xs[i, :, :], stg[:, 0:DI], [bstg], [B_xs])
                dma("sp", D_Btm[i, :, :], stg[:, DI:DI + NG * DS], [bstg], [B_Btm])

    def phase_ssd_core(s, prefix=False, load_prev=None):
        if load_prev is None:
            load_prev = s > 0
        with scope() as loc:
            L = K2(loc)
            xsr = Rot(L, "c0_xs", [128, DI], BF16, 2)
            btmr = Rot(L, "c0_btm", [128, NG * DS], BF16, 2)
            btr = Rot(L, "c0_bt", [128, NG, 128], BF16, 2)
            ctr = Rot(L, "c0_ct", [128, NG, 128], BF16, 2)
            dtTr = Rot(L, "c0_dtT", [64, 128], F32, 2)
            szr = Rot(L, "c0_sz", [128, 32, 128], BF16, 2)
            sm = Rot(L, "c0_sm", [128, 10, 64], F32, 2)
            acTr = Rot(L, "c0_acT", [64, 128], F32, 2)
            achr = Rot(L, "c0_ach", [64, 2, 128], BF16, 2)
            xdtr = Rot(L, "c0_xdt", [128, 64, 64], BF16, 1)
            xddr = Rot(L, "c0_xdd", [128, 64, 64], BF16, 1)
            cbr = Rot(L, "c0_cb", [128, 128], F32, 2)
            decr = Rot(L, "c0_dec", [128, 8, 128], F32, 2)
            mtr = Rot(L, "c0_mt", [128, 8, 128], BF16, 2)
            t1r = Rot(L, "c0_t1", [128, 8, 64], F32, 2)
            t2r = Rot(L, "c0_t2", [128, 8, 64], F32, 2)
            yr = Rot(L, "c0_y", [128, DI], F32, 1)
            yzr = Rot(L, "c0_yz", [128, 32, 128], F32, 1)
            sqr = Rot(L, "c0_sq", [128, 4, 128], BF16, 2)
            rsr = Rot(L, "c0_rs", [128, 1024], F32, 1)
            rtr = Rot(L, "c0_rt", [128, 1024], F32, 1)
            ynr = Rot(L, "c0_yn", [128, 32, 128], BF16, 1)
            prev = sbt(loc, "c0_prev", [128, NG, 512], F32)
            prevb = sbt(loc, "c0_prevb", [128, NG, 512], BF16)
            B_prev = [Buf("prev%d" % g) for g in range(NG)]
            B_prevb = [Buf("prevb%d" % g) for g in range(NG)]
            if not load_prev:
                P.op("dve", lambda e: e.memset(prev[:], 0.0), writes=B_prev)
            else:
                dma("sp", prev[:].rearrange("p a b -> p (a b)"), D_prev, [B_Dprev], B_prev)
            if not prefix:
                for g in range(NG):
                    P.op("act", lambda e, g=g: e.copy(out=prevb[:, g, :], in_=prev[:, g, :]), reads=[B_prev[g]], writes=[B_prevb[g]])
            pre = {}

            def loads(i):
                ts_ = slice(i * 128, (i + 1) * 128)
                xs, bxs = xsr.next()
                btm, bbtm = btmr.next()
                bt, bbt = btr.next()
                ct, bct = ctr.next()
                dtT, bdtT = dtTr.next()
                sz, bsz = szr.next()
                dma("sp", xs[:], D_xs[i, :, :], [B_xs], [bxs])
                dma("sp", btm[:], D_Btm[i, :, :], [B_Btm], [bbtm])
                dma("sp", dtT[:], D_dtT[:, ts_], [B_dtT], [bdtT])
                if not prefix:
                    dma("sp", bt[:], D_BT[:, :, ts_].rearrange("g p t -> p g t"), [B_BT], [bbt])
                    dma("sp", ct[:], D_CT[:, :, ts_].rearrange("g p t -> p g t"), [B_CT], [bct])
                    for c8 in range(0, 32, 8):
                        dma("sp", sz[:, c8:c8 + 8, :], D_sz[c8:c8 + 8, :, ts_].rearrange("c p t -> p c t"), [B_sz], [bsz])
                pre[i] = (xs, bxs, btm, bbtm, bt, bbt, ct, bct, dtT, bdtT, sz, bsz)

            loads(0)
            for i in range(NT):
                ts_ = slice(i * 128, (i + 1) * 128)
                if i + 1 < NT:
                    loads(i + 1)
                xs, bxs, btm, bbtm, bt, bbt, ct, bct, dtT, bdtT, sz, bsz = pre.pop(i)
                m, bm = sm.next()
                pt, pb = bank(0, 6)
                P.op("pe", lambda e, pt=pt, dtT=dtT: e.transpose(out=pt[:, 0:64], in_=dtT[:, :], identity=identf[0:64, 0:64]),
                     reads=[bdtT, Bc], writes=[pb])
                P.op("dve", lambda e, pt=pt, m=m: e.tensor_tensor(out=m[:, 0, :], in0=pt[:, 0:64], in1=r64[:, 0, :], op=ALU.add),
                     reads=[pb, Bc], writes=[bm])
                P.op("act", lambda e, m=m: e.activation(out=m[:, 1, :], in_=m[:, 0, :], func=AF.Exp), reads=[bm], writes=[bm])
                P.op("act", lambda e, m=m: e.activation(out=m[:, 2, :], in_=m[:, 1, :], func=AF.Ln, bias=1.0), reads=[bm], writes=[bm])
                P.op("dve", lambda e, m=m: e.tensor_tensor(out=m[:, 3, :], in0=m[:, 2, :], in1=arep[:], op=ALU.mult),
                     reads=[bm, Bc], writes=[bm])
                pt2, pb2 = bank(0, 6)
                P.op("pe", lambda e, pt2=pt2, m=m: e.matmul(pt2[:, 0:64], lhsT=triuf[:], rhs=m[:, 3, :], start=True, stop=True),
                     reads=[bm, Bc], writes=[pb2])
                P.op("pe", lambda e, pt2=pt2, m=m: e.matmul(pt2[:, 64:128], lhsT=onesf[:], rhs=m[:, 3, :], start=True, stop=True),
                     reads=[bm, Bc], writes=[pb2])
                P.op("dve", lambda e, pt2=pt2, m=m: e.tensor_copy(out=m[:, 4, :], in_=pt2[:, 0:64]), reads=[pb2], writes=[bm])
                P.op("dve", lambda e, pt2=pt2, m=m: e.tensor_scalar(out=m[:, 5, :], in0=pt2[:, 0:64], scalar1=-1.0, scalar2=None, op0=ALU.mult),
                     reads=[pb2], writes=[bm])
                P.op("act", lambda e, pt2=pt2, m=m: e.activation(out=m[:, 6, :], in_=pt2[:, 0:64], func=AF.Exp), reads=[pb2], writes=[bm])
                P.op("act", lambda e, pt2=pt2, m=m: e.activation(out=m[:, 7, :], in_=pt2[:, 64:128], func=AF.Exp), reads=[pb2], writes=[bm])
                P.op("dve", lambda e, pt2=pt2, m=m: e.tensor_tensor(out=m[:, 8, :], in0=pt2[:, 64:128], in1=m[:, 4, :], op=ALU.subtract),
                     reads=[pb2, bm], writes=[bm])
                P.op("act", lambda e, m=m: e.activation(out=m[:, 8, :], in_=m[:, 8, :], func=AF.Exp), reads=[bm], writes=[bm])
                P.op("dve", lambda e, m=m: e.tensor_tensor(out=m[:, 9, :], in0=m[:, 2, :], in1=m[:, 8, :], op=ALU.mult),
                     reads=[bm], writes=[bm])
                xs3 = xs[:].rearrange("p (h d) -> p h d", h=64)
                xdd, bxdd = xddr.next()
                P.op("dve", lambda e, xs3=xs3, xdd=xdd, m=m: e.tensor_tensor(
                    out=xdd[:], in0=xs3, in1=m[:, 9, :].unsqueeze(2).to_broadcast([128, 64, 64]), op=ALU.mult),
                    reads=[bxs, bm], writes=[bxdd])
                if prefix:
                    for g in range(NG):
                        hs = slice(g * 8, (g + 1) * 8)
                        pSt, pbSt = bank(0, 6)
                        P.op("pe", lambda e, pSt=pSt, btm=btm, xdd=xdd, g=g, hs=hs: e.matmul(
                            pSt[:, :], lhsT=btm[:, g * 128:(g + 1) * 128], rhs=xdd[:, hs, :].rearrange("p a b -> p (a b)"), start=True, stop=True),
                            reads=[bbtm, bxdd], writes=[pbSt])
                        pv3 = prev[:, g, :].rearrange("p (a b) -> p a b", a=8)
                        P.op("dve", lambda e, pv3=pv3, m=m, hs=hs: e.tensor_tensor(
                            out=pv3, in0=pv3, in1=m[:, 7, hs].unsqueeze(2).to_broadcast([128, 8, 64]), op=ALU.mult),
                            reads=[B_prev[g], bm], writes=[B_prev[g]])
                        P.op("dve", lambda e, pSt=pSt, g=g: e.tensor_tensor(out=prev[:, g, :], in0=prev[:, g, :], in1=pSt[:, :], op=ALU.add),
                             reads=[B_prev[g], pbSt], writes=[B_prev[g]])
                    continue
                pt3, pb3 = bank(0, 6)
                P.op("pe", lambda e, pt3=pt3, m=m: e.transpose(out=pt3[0:64, 0:128], in_=m[:, 4, :], identity=identf[:]),
                     reads=[bm, Bc], writes=[pb3])
                acT, bacT = acTr.next()
                ach, bach = achr.next()
                P.op("dve", lambda e, pt3=pt3, acT=acT: e.tensor_copy(out=acT[:], in_=pt3[0:64, 0:128]), reads=[pb3], writes=[bacT])
                P.op("dve", lambda e, acT=acT, ach=ach: e.tensor_copy(out=ach[:, 0, :], in_=acT[:]), reads=[bacT], writes=[bach])
                P.op("dve", lambda e, acT=acT, ach=ach: e.tensor_tensor(out=acT[:], in0=acT[:], in1=ach[:, 0, :], op=ALU.subtract),
                     reads=[bacT, bach], writes=[bacT])
                P.op("dve", lambda e, acT=acT, ach=ach: e.tensor_copy(out=ach[:, 1, :], in_=acT[:]), reads=[bacT], writes=[bach])
                xdt, bxdt = xdtr.next()
                P.op("dve", lambda e, xs3=xs3, xdt=xdt, m=m: e.tensor_tensor(
                    out=xdt[:], in0=xs3, in1=m[:, 2, :].unsqueeze(2).to_broadcast([128, 64, 64]), op=ALU.mult),
                    reads=[bxs, bm], writes=[bxdt])
                y, by = yr.next()
                for g in range(NG):
                    hs = slice(g * 8, (g + 1) * 8)
                    pc, pbc = bank(0, 6)
                    P.op("pe", lambda e, pc=pc, bt=bt, ct=ct, g=g: e.matmul(pc[:, 0:128], lhsT=bt[:, g, :], rhs=ct[:, g, :], start=True, stop=True),
                         reads=[bbt, bct], writes=[pbc])
                    cb, bcb = cbr.next()
                    P.op("dve", lambda e, pc=pc, cb=cb: e.tensor_copy(out=cb[:], in_=pc[:, 0:128]), reads=[pbc], writes=[bcb])
                    dec, bdec = decr.next()
                    mt, bmt = mtr.next()
                    for half in range(2):
                        pS, pbS = bank(0, 6)
                        for h4 in range(4):
                            h = half * 4 + h4
                            hh = g * 8 + h
                            osl = slice(h4 * 128, (h4 + 1) * 128)
                            P.op("pe", lambda e, pS=pS, osl=osl: e.matmul(pS[:, osl], lhsT=ident[:], rhs=negm[:], start=True, stop=False),
                                 reads=[Bc], writes=[pbS])
                            P.op("pe", lambda e, pS=pS, osl=osl, hh=hh, ach=ach: e.matmul(pS[:, osl], lhsT=selS[:, hh, :], rhs=ach[:, 0, :], start=False, stop=False),
                                 reads=[Bc, bach], writes=[pbS])
                            P.op("pe", lambda e, pS=pS, osl=osl, hh=hh, ach=ach: e.matmul(pS[:, osl], lhsT=selS[:, hh, :], rhs=ach[:, 1, :], start=False, stop=True),
                                 reads=[Bc, bach], writes=[pbS])
                        for h4 in range(4):
                            h = half * 4 + h4
                            hh = g * 8 + h
                            P.op("act", lambda e, pS=pS, h4=h4, h=h, hh=hh, dec=dec, m=m: e.activation(
                                out=dec[:, h, :], in_=pS[:, h4 * 128:(h4 + 1) * 128], func=AF.Exp, bias=m[:, 5, hh:hh + 1]),
                                reads=[pbS, bm], writes=[bdec])
                    P.op("dve", lambda e, mt=mt, dec=dec, cb=cb: e.tensor_tensor(
                        out=mt[:], in0=dec[:], in1=cb[:].unsqueeze(1).to_broadcast([128, 8, 128]), op=ALU.mult),
                        reads=[bdec, bcb], writes=[bmt])
                    pY, pbY = bank(0, 6)
                    for h in range(8):
                        hh = g * 8 + h
                        P.op("pe", lambda e, pY=pY, h=h, hh=hh, mt=mt, xdt=xdt: e.matmul(
                            pY[:, h * 64:(h + 1) * 64], lhsT=mt[:, h, :], rhs=xdt[:, hh, :], start=True, stop=True),
                            reads=[bmt, bxdt], writes=[pbY])
                    pO, pbO = bank(0, 6)
                    P.op("pe", lambda e, pO=pO, ct=ct, g=g: e.matmul(pO[:, :], lhsT=ct[:, g, :], rhs=prevb[:, g, :], start=True, stop=True),
                         reads=[bct, B_prevb[g]], writes=[pbO])
                    pSt, pbSt = bank(0, 6)
                    P.op("pe", lambda e, pSt=pSt, btm=btm, xdd=xdd, g=g, hs=hs: e.matmul(
                        pSt[:, :], lhsT=btm[:, g * 128:(g + 1) * 128], rhs=xdd[:, hs, :].rearrange("p a b -> p (a b)"), start=True, stop=True),
                        reads=[bbtm, bxdd], writes=[pbSt])
                    t1, bt1 = t1r.next()
                    t2, bt2 = t2r.next()
                    P.op("dve", lambda e, t1=t1, pO=pO, m=m, hs=hs: e.tensor_tensor(
                        out=t1[:], in0=pO[:, :].rearrange("p (a b) -> p a b", a=8),
                        in1=m[:, 6, hs].unsqueeze(2).to_broadcast([128, 8, 64]), op=ALU.mult),
                        reads=[pbO, bm], writes=[bt1])
                    P.op("dve", lambda e, t1=t1, pY=pY: e.tensor_tensor(
                        out=t1[:], in0=t1[:], in1=pY[:, :].rearrange("p (a b) -> p a b", a=8), op=ALU.add),
                        reads=[pbY, bt1], writes=[bt1])
                    P.op("dve", lambda e, t2=t2, xs3=xs3, hs=hs: e.tensor_tensor(
                        out=t2[:], in0=xs3[:, hs, :], in1=r64[:, 2, hs].unsqueeze(2).to_broadcast([128, 8, 64]), op=ALU.mult),
                        reads=[bxs, Bc], writes=[bt2])
                    P.op("dve", lambda e, t1=t1, t2=t2, y=y, g=g: e.tensor_tensor(
                        out=y[:, g * 512:(g + 1) * 512].rearrange("p (a b) -> p a b", a=8), in0=t1[:], in1=t2[:], op=ALU.add),
                        reads=[bt1, bt2], writes=[by])
                    pv3 = prev[:, g, :].rearrange("p (a b) -> p a b", a=8)
                    P.op("dve", lambda e, pv3=pv3, m=m, hs=hs: e.tensor_tensor(
                        out=pv3, in0=pv3, in1=m[:, 7, hs].unsqueeze(2).to_broadcast([128, 8, 64]), op=ALU.mult),
                        reads=[B_prev[g], bm], writes=[B_prev[g]])
                    P.op("dve", lambda e, pSt=pSt, g=g: e.tensor_tensor(out=prev[:, g, :], in0=prev[:, g, :], in1=pSt[:, :], op=ALU.add),
                         reads=[B_prev[g], pbSt], writes=[B_prev[g]])
                    P.op("act", lambda e, g=g: e.copy(out=prevb[:, g, :], in_=prev[:, g, :]), reads=[B_prev[g]], writes=[B_prevb[g]])
                yz, byz = yzr.next()
                rs, brs = rsr.next()
                rt, brt = rtr.next()
                yn, byn = ynr.next()
                for mg in range(8):
                    pT, pbT = bank(0, 4)
                    for j in range(4):
                        c = mg * 4 + j
                        P.op("pe", lambda e, pT=pT, j=j, c=c, y=y: e.transpose(
                            out=pT[:, j * 128:(j + 1) * 128], in_=y[:, c * 128:(c + 1) * 128], identity=identf[:]),
                            reads=[by, Bc], writes=[pbT])
                    P.op("dve", lambda e, pT=pT, mg=mg, yz=yz, sz=sz: e.tensor_tensor(
                        out=yz[:, mg * 4:(mg + 1) * 4, :], in0=pT[:, :].rearrange("p (a b) -> p a b", a=4),
                        in1=sz[:, mg * 4:(mg + 1) * 4, :], op=ALU.mult), reads=[pbT, bsz], writes=[byz])
                    sq, bsq = sqr.next()
                    P.op("act", lambda e, sq=sq, yz=yz, mg=mg: e.activation(out=sq[:], in_=yz[:, mg * 4:(mg + 1) * 4, :], func=AF.Square),
                         reads=[byz], writes=[bsq])
                    pq = ps[4 + mg // 4]
                    pbq = psb[4 + mg // 4]
                    for j in range(4):
                        P.op("pe", lambda e, pq=pq, mg=mg, j=j, sq=sq: e.matmul(
                            pq[:, (mg % 4) * 128:(mg % 4 + 1) * 128], lhsT=onesb[:], rhs=sq[:, j, :], start=(j == 0), stop=(j == 3)),
                            reads=[bsq, Bc], writes=[pbq])
                for hb_ in range(2):
                    P.op("act", lambda e, hb_=hb_, rt=rt: e.activation(out=rt[:, hb_ * 512:(hb_ + 1) * 512], in_=ps[4 + hb_][:, :], func=AF.Sqrt,
                                                                         bias=epsb[:, 0:1], scale=1.0 / 512), reads=[psb[4 + hb_], Bc], writes=[brt])
                P.op("dve", lambda e, rs=rs, rt=rt: e.reciprocal(out=rs[:], in_=rt[:]), reads=[brt], writes=[brs])
                for c in range(32):
                    mg = c // 4
                    P.op("dve", lambda e, c=c, mg=mg, yn=yn, yz=yz, rs=rs: e.scalar_tensor_tensor(
                        out=yn[:, c, :], in0=yz[:, c, :], scalar=nw_s[:, c:c + 1], in1=rs[:, mg * 128:(mg + 1) * 128],
                        op0=ALU.mult, op1=ALU.mult), reads=[byz, brs, Bc], writes=[byn])
                for c8 in range(0, 32, 8):
                    dma("sp", D_yn[c8:c8 + 8, :, ts_].rearrange("c p t -> p c t"), yn[:, c8:c8 + 8, :], [byn], [B_yn])
            if prefix:
                for g in range(NG):
                    P.op("dve", lambda e, g=g: e.tensor_scalar(out=prev[:, g, :], in0=prev[:, g, :], scalar1=role_s[:, 0:1], scalar2=None, op0=ALU.mult),
                         reads=[B_prev[g], Bc], writes=[B_prev[g]])
            dma("sp", D_prev, prev[:].rearrange("p a b -> p (a b)"), B_prev, [B_Dprev])

    def phase_proj_post(s, name, Dsrc, bsrc, KC, W, vidx, CW):
        c0 = s * SG
        with scope() as loc:
            xT, bx = load_xT(loc, name + "_x", Dsrc, bsrc, KC)
            cons = post_consumer(loc, name, None)
            linear_fm(loc, name, [W], KC, 0, D, xT, bx, cons, CW=CW)
        with scope() as loc:
            post_finalize(loc, name, c0, vidx, False)

    def phase_ffn(s, layer):
        c0 = s * SG
        with scope() as loc:
            actT = sbt(loc, "f_act", [128, 44, SG], BF16)
            bact = Buf("actT")
            with scope() as l2:
                hnT, b_hn = norm_fm(l2, c0, 4 + layer, "nf", nhb=1)
                sgr = Rot(K2(l2), "f_sg", [128, 512], F32, 3)

                def cons(co, g, pts, pbs, rows):
                    t, bt = sgr.next()
                    P.op("act", lambda e: e.activation(out=t[:], in_=pts[0][:, :], func=AF.Silu), reads=[pbs[0]], writes=[bt])
                    P.op("dve", lambda e: e.tensor_tensor(out=actT[:, co, g * 512:(g + 1) * 512], in0=t[:], in1=pts[1][:, :], op=ALU.mult),
                         reads=[bt, pbs[1]], writes=[bact])

                linear_fm(l2, "fgu", [ffn_wg[layer], ffn_wu[layer]], 16, 0, FFN, hnT, b_hn, cons, CW=256, nbuf=2)
            with scope() as l2:
                cons2 = post_consumer(l2, "fd", None)
                linear_fm(l2, "fd", [ffn_wd[layer]], 44, 0, D, actT, bact, cons2, CW=128)
        with scope() as loc:
            post_finalize(loc, "fd", c0, 6 + layer, False)

    def phase_ple(s, layer):
        c0 = s * SG
        with scope() as loc:
            L = K2(loc)
            pT = sbt(loc, "pl_pT", [128, 2, SG], BF16)
            bpT = Buf("pT")
            pr = Rot(L, "pl_p", [128, PLE], F32, 2)
            for i in range(NT):
                pt_, bp = pr.next()
                dma("sp", pt_[:], p_in[layer, c0 + i * 128:c0 + (i + 1) * 128, :], [], [bp])
                pt, pb = bank(0, 6)
                for j in range(2):
                    P.op("pe", lambda e, pt=pt, pt_=pt_, j=j: e.transpose(out=pt[:, j * 128:(j + 1) * 128], in_=pt_[:, j * 128:(j + 1) * 128], identity=identf[:]),
                         reads=[bp, Bc], writes=[pb])
                P.op("dve", lambda e, pt=pt, i=i: e.tensor_copy(out=pT[:, :, i * 128:(i + 1) * 128], in_=pt[:, 0:256].rearrange("p (a b) -> p a b", a=2)),
                     reads=[pb], writes=[bpT])
            cons = post_consumer(loc, "plp", None)
            linear_fm(loc, "plp", [ple_wp[layer]], 2, 0, D, pT, bpT, cons, CW=512)
            hbT = sbt(loc, "pl_hb", [128, 16, SG], BF16)
            bhb = Buf("hbT")
            hr = Rot(L, "pl_h", [128, 512], F32, 3)
            for k in range(16):
                for g in range(NGRP):
                    h, bh = hr.next()
                    dma("sp", h[:], D_hT[k, :, c0 + g * 512:c0 + (g + 1) * 512], [B_hT[k]], [bh])
                    P.op("act", lambda e, h=h, k=k, g=g: e.copy(out=hbT[:, k, g * 512:(g + 1) * 512], in_=h[:]), reads=[bh], writes=[bhb])
            gr = Rot(L, "pl_g", [128, 512], F32, 3)

            def consg(co, g, pts, pbs, rows):
                t, bt = gr.next()
                P.op("act", lambda e: e.activation(out=t[:], in_=pts[0][:, :], func=AF.Sigmoid), reads=[pbs[0]], writes=[bt])
                dma("sp", D_g[co, :, g * 512:(g + 1) * 512], t[:], [bt], [B_g[co]])

            linear_fm(loc, "plg", [ple_wg[layer]], 16, 0, D, hbT, bhb, consg, CW=512)
        with scope() as loc:
            post_finalize(loc, "pl", c0, 8 + layer, True)

    def phase_fox_inproj(s):
        c0 = s * SG
        with scope() as loc:
            L = K2(loc)
            hnT, b_hn = norm_fm(loc, c0, 1, "n1")
            qr = Rot(L, "b1_q", [128, 512], BF16, 3)
            vr = Rot(L, "b1_v", [128, 512], BF16, 5)
            str_ = Rot(L, "b1_st", [128, 4, 128], BF16, 3)
            fT = sbt(loc, "b1_fT", [16, SG], F32)
            bfT = Buf("fT")

            def cons(co, g, pts, pbs, rows):
                pt, pb = pts[0], pbs[0]
                cs = slice(g * 512, (g + 1) * 512)
                if co < 16:
                    t, bt = qr.next()
                    P.op("act", lambda e: e.activation(out=t[:], in_=pt[:, :], func=AF.Copy, scale=float(FD) ** -0.5), reads=[pb], writes=[bt])
                    dma("sp", D_QT[co, :, cs], t[:], [bt], [B_QT])
                elif co < 32:
                    t, bt = qr.next()
                    P.op("act", lambda e: e.copy(out=t[:], in_=pt[:, :]), reads=[pb], writes=[bt])
                    if paired:
                        hk = co - 16
                        dma("sp", X_K[hk // 4][(hk % 4) * 128:(hk % 4 + 1) * 128, cs], t[:], [bt], [B_XK[hk // 4]])
                        if hk % 4 == 3 and g == NGRP - 1:
                            j = hk // 4
                            P.op("pool", lambda e, j=j: e.collective_compute("AllGather", ALU.bypass, replica_groups=PAIRS,
                                                                             ins=[X_K[j]], outs=[G_K[j]]),
                                 reads=[B_XK[j]], writes=[B_GK[j]], cc=True)
                    else:
                        dma("sp", D_KT[co - 16, :, c0 + g * 512:c0 + (g + 1) * 512], t[:], [bt], [B_KT])
                elif co < 48:
                    h = co - 32
                    t, bt = vr.next()
                    P.op("act", lambda e: e.copy(out=t[:], in_=pt[:, :]), reads=[pb], writes=[bt])
                    def late():
                        pt2, pb2 = bank(0, 6)
                        pv = pt2[:, :].bitcast(BF16)
                        for j in range(4):
                            P.op("pe", lambda e, j=j: e.transpose(out=pv[:, j * 128:(j + 1) * 128], in_=t[:, j * 128:(j + 1) * 128], identity=ident[:]),
                                 reads=[bt, Bc], writes=[pb2])
                        sg_, bsg = str_.next()
                        P.op("dve", lambda e: e.tensor_copy(out=sg_[:].rearrange("p a b -> p (a b)"), in_=pv[:, 0:512]), reads=[pb2], writes=[bsg])
                        if paired:
                            for jj in range(2):
                                j = 2 * g + jj
                                dma("sp", X_V[j][:, h * 128:(h + 1) * 128].rearrange("(t p) c -> p t c", p=128), sg_[:, 2 * jj:2 * jj + 2, :],
                                    [bsg], [B_XV[j]])
                                if h == FH - 1:
                                    P.op("pool", lambda e, j=j: e.collective_compute("AllGather", ALU.bypass, replica_groups=PAIRS,
                                                                                     ins=[X_V[j]], outs=[G_V[j]]),
                                         reads=[B_XV[j]], writes=[B_GV[j]], cc=True)
                        else:
                            t0 = c0 // 128 + g * 4
                            dma("sp", D_V[t0:t0 + 4, :, h * 128:(h + 1) * 128].rearrange("j p c -> p j c"), sg_[:], [bsg], [B_V])
                    late()
                    return None
                else:
                    P.op("act", lambda e: e.copy(out=fT[:, cs], in_=pt[0:16, :]), reads=[pb], writes=[bfT])

            linear_fm(loc, "b1", [fox_w_in], 16, 0, FOX_IN, hnT, b_hn, cons, CW=512)
            smr = Rot(L, "b1_sm", [128, 4, 16], F32, 2)
            csr = Rot(L, "b1_cs", [128, 16], F32, 2)
            cTr = Rot(L, "b1_cT", [16, 128], F32, 2)
            for i in range(NT):
                ti = (NT + i) if paired else (c0 // 128 + i)
                pt, pb = bank(0, 6)
                P.op("pe", lambda e, pt=pt, i=i: e.transpose(out=pt[:, 0:16], in_=fT[:, i * 128:(i + 1) * 128], identity=identf[0:16, 0:16]),
                     reads=[bfT, Bc], writes=[pb])
                m, bm = smr.next()
                P.op("dve", lambda e, pt=pt, m=m: e.tensor_tensor(out=m[:, 0, :], in0=pt[:, 0:16], in1=r16[:], op=ALU.add), reads=[pb, Bc], writes=[bm])
                P.op("act", lambda e, m=m: e.activation(out=m[:, 1, :], in_=m[:, 0, :], func=AF.Exp, scale=-1.0), reads=[bm], writes=[bm])
                P.op("act", lambda e, m=m: e.activation(out=m[:, 2, :], in_=m[:, 1, :], func=AF.Ln, bias=1.0), reads=[bm], writes=[bm])
                P.op("dve", lambda e, m=m: e.tensor_scalar(out=m[:, 3, :], in0=m[:, 2, :], scalar1=-1.0, scalar2=None, op0=ALU.mult), reads=[bm], writes=[bm])
                pt2, pb2 = bank(0, 6)
                P.op("pe", lambda e, pt2=pt2, m=m: e.matmul(pt2[:, 0:16], lhsT=triuf[:], rhs=m[:, 3, :], start=True, stop=True), reads=[bm, Bc], writes=[pb2])
                P.op("pe", lambda e, pt2=pt2, m=m: e.matmul(pt2[:, 16:32], lhsT=onesf[:], rhs=m[:, 3, :], start=True, stop=True), reads=[bm, Bc], writes=[pb2])
                cs_, bcs = csr.next()
                P.op("dve", lambda e, pt2=pt2, cs_=cs_: e.tensor_tensor(out=cs_[:], in0=pt2[:, 0:16], in1=carryb[:], op=ALU.add),
                     reads=[pb2, B_carry], writes=[bcs])
                P.op("dve", lambda e, cs_=cs_, ti=ti: e.tensor_scalar(out=ncs_all[:, ti, :], in0=cs_[:], scalar1=-1.0, scalar2=None, op0=ALU.mult),
                     reads=[bcs], writes=[B_ncs])
                P.op("dve", lambda e, pt2=pt2: e.tensor_tensor(out=carryb[:], in0=carryb[:], in1=pt2[:, 16:32], op=ALU.add),
                     reads=[pb2, B_carry], writes=[B_carry])
                pt3, pb3 = bank(0, 6)
                P.op("pe", lambda e, pt3=pt3, cs_=cs_: e.transpose(out=pt3[0:16, 0:128], in_=cs_[:], identity=identf[:]), reads=[bcs, Bc], writes=[pb3])
                cT, bcT = cTr.next()
                sl = slice(i * 128, (i + 1) * 128)
                P.op("dve", lambda e, pt3=pt3, cT=cT: e.tensor_copy(out=cT[:], in_=pt3[0:16, 0:128]), reads=[pb3], writes=[bcT])
                P.op("dve", lambda e, cT=cT, sl=sl: e.tensor_copy(out=csThi[:, sl], in_=cT[:]), reads=[bcT], writes=[B_csT])
                P.op("dve", lambda e, cT=cT, sl=sl: e.tensor_tensor(out=cT[:], in0=cT[:], in1=csThi[:, sl], op=ALU.subtract), reads=[bcT, B_csT], writes=[bcT])
                P.op("dve", lambda e, cT=cT, sl=sl: e.tensor_copy(out=csTlo[:, sl], in_=cT[:]), reads=[bcT], writes=[B_csT])
            if paired:
                sfx = sbt(loc, "b1_sfx", [128, NT, 16], F32)
                bsfx = Buf("sfx")
                P.op("dve", lambda e: e.tensor_tensor(out=sfx[:], in0=ncs_all[:, NT:2 * NT, :],
                                                      in1=carryb[:].unsqueeze(1).to_broadcast([128, NT, 16]), op=ALU.add),
                     reads=[B_ncs, B_carry], writes=[bsfx])
                dma("sp", X_S, sfx[:].rearrange("p a b -> p (a b)"), [bsfx], [B_XS])
                P.op("pool", lambda e: e.collective_compute("AllGather", ALU.bypass, replica_groups=PAIRS, ins=[X_S], outs=[G_S]),
                     reads=[B_XS], writes=[B_GS], cc=True)
                dma("sp", ncs_all[:, 0:NT, :].rearrange("p a b -> p (a b)"), G_S[0:128, :], [B_GS], [B_ncs])
                P.op("dve", lambda e: e.tensor_scalar(out=ncs_all[:, 0:NT, :], in0=ncs_all[:, 0:NT, :], scalar1=role_s[:, 1:2], scalar2=None, op0=ALU.add),
                     reads=[B_ncs, Bc], writes=[B_ncs])

    def phase_fox_attn(s):
        c0 = SG if paired else s * SG
        nkt_all = (c0 + SG) // 128
        with scope() as loc:
            L = K2(loc)
            QT, bQ = load_xT(loc, "c1_Q", D_QT, B_QT, FH)
            oT = sbt(loc, "c1_oT", [128, FH, SG], BF16)
            boT = Buf("oT")
            ktr = Rot(L, "c1_kt", [128, nkt_all * 128], BF16, 2)
            vtr = Rot(L, "c1_vt", [128, nkt_all, 128], BF16, 2)
            cqr = Rot(L, "c1_cq", [128, 512], F32, 2)
            tmr = Rot(L, "c1_tm", [128, 512], F32, 3)
            pmr = Rot(L, "c1_pm", [128, 512], BF16, 5)
            rir = Rot(L, "c1_ri", [128, 512], F32, 2)
            for h in range(FH):
                kt_, bkt = ktr.next()
                vt_, bvt = vtr.next()
                if paired:
                    rs_ = slice((h % 4) * 128, (h % 4 + 1) * 128)
                    dma("sp", kt_[:, SG:2 * SG], X_K[h // 4][rs_, :], [B_XK[h // 4]], [bkt])
                    dma("sp", kt_[:, 0:SG], G_K[h // 4][rs_, :], [B_GK[h // 4]], [bkt])
                    for j in range(4):
                        dma("sp", vt_[:, NT + 2 * j:NT + 2 * j + 2, :], X_V[j][:, h * 128:(h + 1) * 128].rearrange("(t p) d -> p t d", p=128),
                            [B_XV[j]], [bvt])
                        dma("sp", vt_[:, 2 * j:2 * j + 2, :], G_V[j][0:256, h * 128:(h + 1) * 128].rearrange("(t p) d -> p t d", p=128),
                            [B_GV[j]], [bvt])
                else:
                    dma("sp", kt_[:], D_KT[h, :, 0:nkt_all * 128], [B_KT], [bkt])
                    for j0 in range(0, nkt_all, 8):
                        dma("sp", vt_[:, j0:j0 + 8, :], D_V[j0:j0 + 8, :, h * 128:(h + 1) * 128].rearrange("j p d -> p j d"), [B_V], [bvt])
                for G in range(NGRP):
                    q0 = c0 + G * 512
                    qs = slice(G * 512, (G + 1) * 512)
                    pq, pbq = ps[3], psb[3]
                    P.op("pe", lambda e, pq=pq, h=h, qs=qs: e.matmul(pq[:, :], lhsT=selF[:, h, :], rhs=csThi[:, qs], start=True, stop=False),
                         reads=[Bc, B_csT], writes=[pbq])
                    P.op("pe", lambda e, pq=pq, h=h, qs=qs: e.matmul(pq[:, :], lhsT=selF[:, h, :], rhs=csTlo[:, qs], start=False, stop=True),
                         reads=[Bc, B_csT], writes=[pbq])
                    cq, bcq = cqr.next()
                    P.op("act", lambda e, cq=cq, pq=pq: e.copy(out=cq[:], in_=pq[:, :]), reads=[pbq], writes=[bcq])
                    nkt = (q0 + 512) // 128
                    pO, pbO = ps[4 + (G % 2) * 2], psb[4 + (G % 2) * 2]
                    pR, pbR = ps[5 + (G % 2) * 2], psb[5 + (G % 2) * 2]
                    order = (list(range(NT, nkt)) + list(range(NT))) if paired else list(range(nkt))
                    stage2 = []
                    for ki, kt in enumerate(order):
                        lo_ = 0 if kt * 128 < q0 else (kt * 128 - q0)
                        cs = slice(lo_, 512)
                        pS, pbS = bank(0, 3)
                        P.op("pe", lambda e, pS=pS, kt_=kt_, kt=kt, h=h, G=G, lo_=lo_, cs=cs: e.matmul(
                            pS[:, cs], lhsT=kt_[:, kt * 128:(kt + 1) * 128], rhs=QT[:, h, G * 512 + lo_:(G + 1) * 512], start=True, stop=True),
                            reads=[bkt, bQ], writes=[pbS])
                        tm, btm = tmr.next()
                        P.op("dve", lambda e, tm=tm, pS=pS, cs=cs, kt=kt, h=h, cq=cq: e.scalar_tensor_tensor(
                            out=tm[:, cs], in0=pS[:, cs], scalar=ncs_all[:, kt, h:h + 1], in1=cq[:, cs], op0=ALU.add, op1=ALU.add),
                            reads=[pbS, B_ncs, bcq], writes=[btm])
                        pm, bpm = pmr.next()
                        P.op("act", lambda e, pm=pm, tm=tm, cs=cs: e.activation(out=pm[:, cs], in_=tm[:, cs], func=AF.Exp), reads=[btm], writes=[bpm])
                        if kt * 128 >= q0:
                            P.op("dve", lambda e, pm=pm, lo_=lo_: e.tensor_tensor(out=pm[:, lo_:lo_ + 128], in0=pm[:, lo_:lo_ + 128], in1=triub[:], op=ALU.mult),
                                 reads=[bpm, Bc], writes=[bpm])
                        def st2(pO=pO, pR=pR, vt_=vt_, kt=kt, pm=pm, bpm=bpm, cs=cs, nkt=nkt, ki=ki, pbO=pbO, pbR=pbR, bvt=bvt):
                            P.op("pe", lambda e: e.matmul(
                                pO[:, cs], lhsT=vt_[:, kt, :], rhs=pm[:, cs], start=(ki == 0), stop=(ki == nkt - 1)),
                                reads=[bvt, bpm], writes=[pbO])
                            P.op("pe", lambda e: e.matmul(
                                pR[:, cs], lhsT=onesb[:], rhs=pm[:, cs], start=(ki == 0), stop=(ki == nkt - 1)),
                                reads=[Bc, bpm], writes=[pbR])
                        stage2.append(st2)
                        while len(stage2) > ALAG:
                            stage2.pop(0)()
                    for f_ in stage2:
                        f_()
                    ri, bri = rir.next()
                    P.op("dve", lambda e, ri=ri, pR=pR: e.reciprocal(out=ri[:], in_=pR[:, :]), reads=[pbR], writes=[bri])
                    P.op("dve", lambda e, ri=ri, pO=pO, h=h, qs=qs: e.tensor_tensor(out=oT[:, h, qs], in0=pO[:, :], in1=ri[:], op=ALU.mult),
                         reads=[pbO, bri], writes=[boT])
            for k0 in range(0, FH, 8):
                dma("sp", D_oT[k0:k0 + 8, :, :].rearrange("k p t -> p k t"), oT[:, k0:k0 + 8, :], [boT], [B_oT])

    for s in range(nsg):
        phase_input(s)
    if paired:
        assert nsg == 1
        phase_input(0, src=xp_in, Ddst=D_hTp, Bdst=B_hTp)
        phase_ssd_inproj(0, prefix=True)
        phase_ssd_core(0, prefix=True, load_prev=False)
    for s in range(nsg):
        phase_ssd_inproj(s)
        phase_ssd_core(s, load_prev=(True if paired else None))
        phase_proj_post(s, "d0", D_yn, B_yn, 32, ssd_w_out, 2, 256)
        phase_ffn(s, 0)
        phase_ple(s, 0)
    for s in range(nsg):
        phase_fox_inproj(s)
        phase_fox_attn(s)
        phase_proj_post(s, "d1", D_oT, B_oT, 16, fox_w_out, 3, 512)
        phase_ffn(s, 1)
        phase_ple(s, 1)
    for s in range(nsg):
        phase_output(s)
    P.op("sp", lambda e: None, reads=[B_out])
    P.emit(st)
    st.close()
    return nc


_CACHE = {}


def _get_program(nsg):
    if nsg not in _CACHE:
        _CACHE[nsg] = build_program(nsg)
    return _CACHE[nsg]


def _fm(v):
    return np.ascontiguousarray(v.reshape(-1, 128).T)


def make_in_maps(inputs, core_batches, tok_slices):
    f = lambda a: np.ascontiguousarray(np.asarray(a, dtype=np.float32))
    vec_fm = np.stack([_fm(f(inputs[n])[i]) for n in ("norm_mix_pre", "norm_mix_post", "norm_ffn_pre", "norm_ffn_post", "ple_norm")
                       for i in range(2)], axis=1)
    ssd_nw = _fm(f(inputs["ssd_norm_w"])[0])
    cw = f(inputs["ssd_conv_w"])[0]
    conv_w = np.ascontiguousarray(cw.reshape(4, 48, 128).transpose(2, 1, 0))
    conv_b = _fm(f(inputs["ssd_conv_b"])[0])
    rep64 = np.ascontiguousarray(np.broadcast_to(
        np.stack([f(inputs["ssd_dt_bias"])[0], f(inputs["ssd_a_log"])[0], f(inputs["ssd_d"])[0]], 0)[None], (128, 3, 64)))
    rep16 = np.ascontiguousarray(np.broadcast_to(f(inputs["fox_b_f"])[0][None], (128, 16)))
    shared = {
        "vec_fm": np.ascontiguousarray(vec_fm), "ssd_nw": ssd_nw, "conv_w": conv_w, "conv_b": conv_b,
        "rep64": rep64, "rep16": rep16,
        "ssd_w_in": f(inputs["ssd_w_in"])[0], "ssd_w_out": f(inputs["ssd_w_out"])[0],
        "fox_w_in": f(inputs["fox_w_in"])[0], "fox_w_out": f(inputs["fox_w_out"])[0],
        "ffn_w_gate": f(inputs["ffn_w_gate"]), "ffn_w_up": f(inputs["ffn_w_up"]), "ffn_w_down": f(inputs["ffn_w_down"]),
        "ple_w_proj": f(inputs["ple_w_proj"]), "ple_w_gate": f(inputs["ple_w_gate"]),
    }
    x = f(inputs["x"])
    p = f(inputs["p"])
    maps = []
    for b, ts in zip(core_batches, tok_slices):
        m = dict(shared)
        m["x"] = np.ascontiguousarray(x[b, ts])
        m["p"] = np.ascontiguousarray(p[:, b, ts])
        maps.append(m)
    return maps


def make_paired_maps(inputs):
    core_batches = [c // 2 for c in range(8)]
    toks = [slice((c % 2) * SG, (c % 2 + 1) * SG) for c in range(8)]
    maps = make_in_maps(inputs, core_batches, toks)
    x = np.asarray(inputs["x"], dtype=np.float32)
    for c, m in enumerate(maps):
        m["xp"] = np.ascontiguousarray(x[c // 2, 0:SG])
        r = np.zeros((128, 2), np.float32)
        r[:, 0] = float(c % 2)
        r[:, 1] = 0.0 if c % 2 else -30000.0
        m["role"] = r
    return maps


def kernel(**inputs):
    if "paired" not in _CACHE:
        _CACHE["paired"] = build_program(1, paired=True)
    nc = _CACHE["paired"]
    maps = make_paired_maps(inputs)
    res = run_bass_kernel_spmd(nc, maps, core_ids=list(range(8)))
    out = np.empty((BATCH, SEQ, D), np.float32)
    for c in range(8):
        out[c // 2, (c % 2) * SG:(c % 2 + 1) * SG] = np.asarray(res.results[c]["out"], dtype=np.float32)
    return out
```

```python
import numpy as np
from contextlib import ExitStack
import concourse.bass as bass
import concourse.mybir as mybir
from concourse.bass_utils import run_bass_kernel_spmd

F32 = mybir.dt.float32
BF16 = mybir.dt.bfloat16
AF = mybir.ActivationFunctionType
ALU = mybir.AluOpType

D = 2048
SEQ = 2048
BATCH = 4
DI = 4096
NH = 64
HP = 64
NG = 8
DS = 128
CONV_DIM = 6144
SSD_IN = 10304
FH = 16
FD = 128
FOX_IN = 6160
FFN = 5632
PLE = 256
EPS = 1e-6
DEFER = 2
SG = 1024
ALAG = 4
NGRP = SG // 512
NT = SG // 128


_P = [None]


class Buf:
    __slots__ = ("name", "w", "wd", "r", "rd", "excl")

    def __init__(self, name="", excl=False):
        self.name = name
        self.w = {}
        self.wd = []
        self.r = {}
        self.rd = []
        self.excl = excl
        p = _P[0]
        if p is not None and p.front_c is not None:
            self.r = dict(p.front_c)
            self.rd = list(p.front_d)


class Op:
    __slots__ = ("eng", "fn", "deps", "dma", "needed", "sig", "cc")

    def __init__(self, eng, fn, dma):
        self.eng = eng
        self.fn = fn
        self.dma = dma
        self.deps = []
        self.needed = False
        self.sig = None
        self.cc = False


ENGS = ("pe", "act", "dve", "pool", "sp")
DMA_K = 8


class Prog:
    def __init__(self, nc):
        self.nc = nc
        self.by_eng = {e: [] for e in ENGS}
        self.dma_hist = {e: [] for e in ENGS}
        self.nops = 0
        self.front_c = None
        self.front_d = None
        _P[0] = self

    def mark(self):
        fc = {}
        fd = []
        for e in ENGS:
            for o in reversed(self.by_eng[e]):
                if not o.dma:
                    fc[e] = o
                    break
            fd.extend(self.dma_hist[e][-DMA_K:])
        self.front_c = fc
        self.front_d = fd

    def op(self, eng, fn, reads=(), writes=(), dma=False, extra_deps=(), cc=False):
        o = Op(eng, fn, dma or cc)
        o.cc = cc
        is_cc = cc
        dma = dma or cc
        deps = {}

        def add(d, raw):
            if d is o:
                return
            if (not d.dma) and (not dma) and d.eng == eng:
                if eng == "pe" or not raw:
                    return
            deps[id(d)] = d

        for b in reads:
            for d in b.w.values():
                add(d, True)
            for d in b.wd:
                add(d, True)
            if b.excl:
                for d in b.r.values():
                    add(d, False)
                for d in b.rd:
                    add(d, False)
        for b in writes:
            for d in b.w.values():
                add(d, False)
            for d in b.wd:
                add(d, False)
            for d in b.r.values():
                add(d, False)
            for d in b.rd:
                add(d, False)
        for d in extra_deps:
            if d is not None:
                deps[id(d)] = d
        if dma and not is_cc:
            h = self.dma_hist[eng]
            if len(h) >= DMA_K:
                d = h[len(h) - DMA_K]
                deps[id(d)] = d
            h.append(o)
        o.deps = list(deps.values())
        for d in o.deps:
            if not d.dma:
                d.needed = True
        for b in reads:
            if b.excl:
                b.w = {}
                b.wd = []
                b.r = {}
                b.rd = []
                if dma:
                    b.wd.append(o)
                else:
                    b.w[eng] = o
            else:
                if dma:
                    b.rd.append(o)
                else:
                    b.r[eng] = o
        for b in writes:
            if b.r or b.rd:
                b.w = {}
                b.wd = []
                b.r = {}
                b.rd = []
            if dma:
                b.wd.append(o)
            else:
                b.w[eng] = o
        self.by_eng[eng].append(o)
        self.nops += 1
        return o

    def emit(self, stack):
        nc = self.nc
        csem = {e: stack.enter_context(nc.semaphore("c_" + e)) for e in ("pe", "act", "dve", "pool")}
        dsem = {e: [stack.enter_context(nc.semaphore("d_%s%d" % (e, i))) for i in range(DMA_K)]
                for e in ENGS if self.dma_hist[e]}
        for e in ENGS:
            cnt = 0
            nd = 0
            for o in self.by_eng[e]:
                if o.cc:
                    o.sig = (stack.enter_context(nc.semaphore("cc%d" % id(o))), 1)
                elif o.dma:
                    o.sig = (dsem[e][nd % DMA_K], 16 * (nd // DMA_K + 1))
                    nd += 1
                elif o.needed:
                    cnt += 1
                    o.sig = (csem[e], cnt)
        block = stack.enter_context(nc.Block())
        hook = {"pe": block.tensor, "act": block.scalar, "dve": block.vector,
                "pool": block.gpsimd, "sp": block.sync}

        def make(e):
            def body(eng):
                waited = {}
                for o in self.by_eng[e]:
                    for d in o.deps:
                        if d.sig is None:
                            continue
                        s, v = d.sig
                        k = id(s)
                        if waited.get(k, 0) < v:
                            eng.wait_ge(s, v)
                            waited[k] = v
                    ins = o.fn(eng)
                    if ins is None:
                        continue
                    if o.cc:
                        ins.then_inc(o.sig[0])
                    elif o.dma:
                        ins.then_inc(o.sig[0], 16)
                    elif o.sig is not None:
                        ins.then_inc(o.sig[0], 1)
            return body

        for e in ENGS:
            if self.by_eng[e]:
                hook[e](make(e))


class Rot:
    def __init__(self, K, name, shape, dtype, n):
        self.t = [K.sb(name + str(i), shape, dtype) for i in range(n)]
        self.b = [Buf(name + str(i)) for i in range(n)]
        self.i = 0

    def next(self):
        i = self.i % len(self.t)
        self.i += 1
        return self.t[i], self.b[i]


class Kern:
    pass


def build_program(nsg, debug=(), paired=False):
    TC = nsg * SG
    nc = bass.Bass("TRN2", target_bir_lowering=False)
    K = Kern()
    K.nc = nc
    st = ExitStack()
    K.st = st
    P = Prog(nc)
    K.P = P

    def ein(name, shape, dt=F32):
        return nc.dram_tensor(name, list(shape), dt, kind="ExternalInput").ap()

    def scratch(name, shape, dt):
        kind = "ExternalOutput" if name in debug else "Internal"
        return nc.dram_tensor(name, list(shape), dt, kind=kind).ap()

    x_in = ein("x", [TC, D])
    if paired:
        xp_in = ein("xp", [SG, D])
        role_in = ein("role", [128, 2])
    p_in = ein("p", [2, TC, PLE])
    vec_fm = ein("vec_fm", [128, 10, 16])
    ssd_nw = ein("ssd_nw", [128, 32])
    conv_w = ein("conv_w", [128, 48, 4])
    conv_b = ein("conv_b", [128, 48])
    rep64 = ein("rep64", [128, 3, 64])
    rep16 = ein("rep16", [128, 16])
    ssd_w_in = ein("ssd_w_in", [D, SSD_IN])
    ssd_w_out = ein("ssd_w_out", [DI, D])
    fox_w_in = ein("fox_w_in", [D, FOX_IN])
    fox_w_out = ein("fox_w_out", [D, D])
    ffn_wg = ein("ffn_w_gate", [2, D, FFN])
    ffn_wu = ein("ffn_w_up", [2, D, FFN])
    ffn_wd = ein("ffn_w_down", [2, FFN, D])
    ple_wp = ein("ple_w_proj", [2, PLE, D])
    ple_wg = ein("ple_w_gate", [2, D, D])
    out = nc.dram_tensor("out", [TC, D], F32, kind="ExternalOutput").ap()

    D_hT = scratch("D_hT", [16, 128, TC], F32)
    D_mix = scratch("D_mix", [16, 128, SG], F32)
    D_g = scratch("D_g", [16, 128, SG], F32)
    D_sz = scratch("D_sz", [32, 128, SG], BF16)
    D_xs = scratch("D_xs", [NT, 128, DI], BF16)
    D_BT = scratch("D_BT", [NG, 128, SG], BF16)
    D_CT = scratch("D_CT", [NG, 128, SG], BF16)
    D_Btm = scratch("D_Btm", [NT, 128, NG * DS], BF16)
    D_dtT = scratch("D_dtT", [64, SG], F32)
    D_yn = scratch("D_yn", [32, 128, SG], BF16)
    D_QT = scratch("D_QT", [FH, 128, SG], BF16)
    D_KT = scratch("D_KT", [FH, 128, TC], BF16)
    D_V = scratch("D_V", [TC // 128, 128, D], BF16)
    D_oT = scratch("D_oT", [FH, 128, SG], BF16)
    D_prev = scratch("D_prev", [128, NG * 512], F32)
    D_xbT = scratch("D_xbT", [40, 128, SG], BF16)
    B_xbT = Buf("xbT")
    if paired:
        D_hTp = scratch("D_hTp", [16, 128, SG], F32)
        B_hTp = [Buf("hTp%d" % k) for k in range(16)]
        X_K = [scratch("X_K%d" % j, [512, SG], BF16) for j in range(4)]
        G_K = [scratch("G_K%d" % j, [1024, SG], BF16) for j in range(4)]
        X_V = [scratch("X_V%d" % j, [256, D], BF16) for j in range(4)]
        G_V = [scratch("G_V%d" % j, [512, D], BF16) for j in range(4)]
        X_S = scratch("X_S", [128, 128], F32)
        G_S = scratch("G_S", [256, 128], F32)
        B_XK = [Buf("XK%d" % j) for j in range(4)]
        B_GK = [Buf("GK%d" % j) for j in range(4)]
        B_XV = [Buf("XV%d" % j) for j in range(4)]
        B_GV = [Buf("GV%d" % j) for j in range(4)]
        B_XS = Buf("XS")
        B_GS = Buf("GS")
        PAIRS = [[0, 1], [2, 3], [4, 5], [6, 7]]
    B_hT = [Buf("hT%d" % k) for k in range(16)]
    B_mix = [Buf("mix%d" % k) for k in range(16)]
    B_g = [Buf("g%d" % k) for k in range(16)]
    B_sz, B_xs, B_BT, B_CT, B_Btm, B_dtT, B_yn = (Buf(n) for n in ("sz", "xs", "BT", "CT", "Btm", "dtT", "yn"))
    B_QT, B_KT, B_V, B_oT, B_out = (Buf(n) for n in ("QT", "KT", "V", "oT", "out"))

    uniq = [0]

    def sbt(stk, name, shape, dt):
        uniq[0] += 1
        return stk.enter_context(nc.sbuf_tensor("%s_%d" % (name, uniq[0]), list(shape), dt))

    def sb(name, shape, dt):
        return sbt(st, name, shape, dt)

    class scope:
        def __enter__(self):
            self.stk = ExitStack()
            return self.stk

        def __exit__(self, *a):
            self.stk.close()
            P.mark()
            return False

    K.sb = sb

    ps = [st.enter_context(nc.psum_tensor("ps%d" % i, [128, 512], F32)) for i in range(8)]
    psb = [Buf("bank%d" % i, excl=True) for i in range(8)]
    bank_rr = [0]

    def bank(lo=0, hi=8):
        n = hi - lo
        i = lo + bank_rr[0] % n
        bank_rr[0] += 1
        return ps[i], psb[i]

    identf = sb("identf", [128, 128], F32)
    ident = sb("ident", [128, 128], BF16)
    onesf = sb("onesf", [128, 128], F32)
    onesb = sb("onesb", [128, 128], BF16)
    triuf = sb("triuf", [128, 128], F32)
    triub = sb("triub", [128, 128], BF16)
    negm = sb("negm", [128, 128], BF16)
    negf = sb("negf", [128, 128], F32)
    selS = sb("selS", [64, 64, 128], BF16)
    selF = sb("selF", [16, 16, 128], BF16)
    vecs = sb("vecs", [128, 10, 16], F32)
    nw_s = sb("nw_s", [128, 32], F32)
    cw_s = sb("cw_s", [128, 48, 4], F32)
    cb_s = sb("cb_s", [128, 48], F32)
    r64 = sb("r64", [128, 3, 64], F32)
    r16 = sb("r16", [128, 16], F32)
    arep = sb("arep", [128, 64], F32)
    halo = sb("halo", [128, 48, 3], F32)
    carryb = sb("carryb", [128, 16], F32)
    ncs_all = sb("ncs_all", [128, (2 * SG if paired else TC) // 128, 16], F32)
    role_s = sb("role_s", [128, 2], F32)
    csThi = sb("csThi", [16, SG], BF16)
    csTlo = sb("csTlo", [16, SG], BF16)
    Bc = Buf("consts")
    B_halo = Buf("halo")
    B_Dprev = Buf("Dprev")
    B_carry = Buf("carry")
    B_ncs = Buf("ncs")
    B_csT = Buf("csT")

    def pool_c(fn, w=(Bc,), r=()):
        return P.op("pool", fn, reads=r, writes=w)

    pool_c(lambda e: e.memset(identf[:], 0.0))
    pool_c(lambda e: e.affine_select(out=identf[:], in_=identf[:], pattern=[[-1, 128]],
                                     compare_op=ALU.not_equal, fill=1.0, base=0, channel_multiplier=1))
    pool_c(lambda e: e.memset(onesf[:], 1.0))
    pool_c(lambda e: e.memset(triuf[:], 1.0))
    pool_c(lambda e: e.affine_select(out=triuf[:], in_=triuf[:], pattern=[[1, 128]],
                                     compare_op=ALU.is_ge, fill=0.0, base=0, channel_multiplier=-1))
    pool_c(lambda e: e.memset(negf[:], 0.0))
    pool_c(lambda e: e.affine_select(out=negf[:], in_=negf[:], pattern=[[1, 128]],
                                     compare_op=ALU.is_ge, fill=-30000.0, base=0, channel_multiplier=-1))
    pool_c(lambda e: e.memset(selS[:], 0.0))
    pool_c(lambda e: e.affine_select(out=selS[:], in_=selS[:], pattern=[[-1, 64], [0, 128]],
                                     compare_op=ALU.not_equal, fill=1.0, base=0, channel_multiplier=1))
    pool_c(lambda e: e.memset(halo[:], 0.0), w=(B_halo,))
    pool_c(lambda e: e.memset(carryb[:], 0.0), w=(B_carry,))
    P.op("dve", lambda e: e.tensor_copy(out=ident[:], in_=identf[:]), reads=[Bc], writes=[Bc])
    P.op("dve", lambda e: e.tensor_copy(out=onesb[:], in_=onesf[:]), reads=[Bc], writes=[Bc])
    P.op("dve", lambda e: e.tensor_copy(out=triub[:], in_=triuf[:]), reads=[Bc], writes=[Bc])
    P.op("dve", lambda e: e.tensor_copy(out=negm[:], in_=negf[:]), reads=[Bc], writes=[Bc])
    P.op("dve", lambda e: e.tensor_copy(out=selF[:], in_=selS[0:16, 0:16, :]), reads=[Bc], writes=[Bc])
    if paired:
        P.op("sp", lambda e: e.dma_start(out=role_s[:], in_=role_in), writes=[Bc], dma=True)
    for dst, src in ((vecs, vec_fm), (nw_s, ssd_nw), (cw_s, conv_w), (cb_s, conv_b), (r64, rep64), (r16, rep16)):
        P.op("sp", lambda e, dst=dst, src=src: e.dma_start(out=dst[:], in_=src), writes=[Bc], dma=True)
    P.op("act", lambda e: e.activation(out=arep[:], in_=r64[:, 1, :], func=AF.Exp), reads=[Bc], writes=[Bc])
    P.op("dve", lambda e: e.tensor_scalar(out=arep[:], in0=arep[:], scalar1=-1.0, scalar2=None, op0=ALU.mult),
         reads=[Bc], writes=[Bc])

    def dma(q, out_ap, in_ap, reads, writes):
        return P.op(q, lambda e: e.dma_start(out=out_ap, in_=in_ap), reads=reads, writes=writes, dma=True)

    def rstd_from_ssq(ss, bss, dst, bdst, cols, inv_n, tmp_rot):
        tmp, btmp = tmp_rot.next()
        n = ss.shape[1]
        P.op("act", lambda e: e.activation(out=tmp[:, 0:n], in_=ss, func=AF.Sqrt, bias=epsb[:, 0:1], scale=inv_n),
             reads=[bss, Bc], writes=[btmp])
        P.op("dve", lambda e: e.reciprocal(out=dst[:, cols], in_=tmp[:, 0:n]), reads=[btmp], writes=[bdst])

    epsb = sb("epsb", [128, 1], F32)
    pool_c(lambda e: e.memset(epsb[:], EPS))

    def norm_fm(stk, c0, vidx, name, nhb=2, Dsrc=None, Bsrc=None):
        hnT = sbt(stk, name + "_hnT", [128, 16, SG], BF16)
        b_hn = Buf(name + "_hnT")
        if Dsrc is None:
            Dsrc, Bsrc = D_hT, B_hT
        with scope() as loc:
            hb = [sbt(loc, name + "_hb%d" % i, [128, 16, 512], F32) for i in range(nhb)]
            bhb = [Buf("hb%d" % i) for i in range(nhb)]
            sq = [sbt(loc, name + "_sq%d" % i, [128, 512], BF16) for i in range(3)]
            bsq = [Buf("sq%d" % i) for i in range(3)]
            rs = sbt(loc, name + "_rs", [128, SG], F32)
            brs = Buf("rs")
            tr = Rot(K2(loc), name + "_tmp", [128, 512], F32, 2)
            for g in range(NGRP):
                cs = slice(c0 + g * 512, c0 + (g + 1) * 512)
                for k in range(16):
                    dma("sp", hb[g % nhb][:, k, :], Dsrc[k, :, cs], [Bsrc[k]], [bhb[g % nhb]])
                pt, pb = bank()
                for k in range(16):
                    i = (g * 16 + k) % 3
                    P.op("act", lambda e, i=i, k=k, g=g: e.activation(out=sq[i][:], in_=hb[g % nhb][:, k, :], func=AF.Square),
                         reads=[bhb[g % nhb]], writes=[bsq[i]])
                    P.op("pe", lambda e, i=i, k=k, pt=pt: e.matmul(pt[:, :], lhsT=onesb[:], rhs=sq[i][:], start=(k == 0), stop=(k == 15)),
                         reads=[bsq[i], Bc], writes=[pb])
                rstd_from_ssq(pt[:, :], pb, rs, brs, slice(g * 512, (g + 1) * 512), 1.0 / D, tr)
                for k in range(16):
                    P.op("dve", lambda e, k=k, g=g: e.scalar_tensor_tensor(
                        out=hnT[:, k, g * 512:(g + 1) * 512], in0=hb[g % nhb][:, k, :], scalar=vecs[:, vidx, k:k + 1],
                        in1=rs[:, g * 512:(g + 1) * 512], op0=ALU.mult, op1=ALU.mult),
                        reads=[bhb[g % nhb], brs, Bc], writes=[b_hn])
        return hnT, b_hn

    class K2:
        def __init__(self, stk):
            self.stk = stk

        def sb(self, name, shape, dt):
            return sbt(self.stk, name, shape, dt)

    def linear_fm(stk, name, Ws, KC, col0, ncols, xT, b_x, consumer, CW, nbuf=3, lo=0, hi=6):
        nW = len(Ws)
        wb = [sbt(stk, "%s_w%d" % (name, i), [128, KC, CW], BF16) for i in range(nbuf * nW)]
        bw = [Buf("%s_w%d" % (name, i)) for i in range(nbuf * nW)]
        nblk = (ncols + CW - 1) // CW
        KS = 8 if KC % 8 == 0 else (11 if KC % 11 == 0 else KC)
        pend = []
        for b in range(nblk):
            c0 = col0 + b * CW
            cw = min(CW, ncols - b * CW)
            slots = []
            for wi, W in enumerate(Ws):
                s = (b % nbuf) * nW + wi
                slots.append(s)
                for k0 in range(0, KC, KS):
                    dma("pool", wb[s][:, k0:k0 + KS, 0:cw],
                        W[k0 * 128:(k0 + KS) * 128, c0:c0 + cw].rearrange("(k p) c -> p k c", p=128), [], [bw[s]])
            for cc in range(0, cw, 128):
                rows = min(128, cw - cc)
                co = (col0 + b * CW + cc) // 128
                for g in range(NGRP):
                    pts = []
                    pbs = []
                    for wi in range(nW):
                        pt, pb = bank(lo, hi)
                        pts.append(pt)
                        pbs.append(pb)
                        s = slots[wi]
                        for k in range(KC):
                            P.op("pe", lambda e, s=s, k=k, cc=cc, rows=rows, pt=pt, g=g: e.matmul(
                                pt[0:rows, :], lhsT=wb[s][:, k, cc:cc + rows], rhs=xT[:, k, g * 512:(g + 1) * 512],
                                start=(k == 0), stop=(k == KC - 1)), reads=[bw[s], b_x], writes=[pb])
                    late = consumer(co, g, pts, pbs, rows)
                    if late is not None:
                        pend.append(late)
                    while len(pend) > DEFER:
                        pend.pop(0)()
        for f_ in pend:
            f_()

    def post_consumer(stk, name, ssb):
        sq = Rot(K2(stk), name + "_psq", [128, 512], BF16, 5)
        mb = Rot(K2(stk), name + "_pmb", [128, 512], F32, 3)

        def cons(co, g, pts, pbs, rows, nco=16):
            pt, pb = pts[0], pbs[0]
            s, bs = sq.next()
            P.op("act", lambda e: e.activation(out=s[:], in_=pt[:, :], func=AF.Square), reads=[pb], writes=[bs])
            m, bm = mb.next()
            P.op("dve", lambda e: e.tensor_copy(out=m[:], in_=pt[:, :]), reads=[pb], writes=[bm])
            dma("sp", D_mix[co, :, g * 512:(g + 1) * 512], m[:], [bm], [B_mix[co]])

            def late():
                P.op("pe", lambda e: e.matmul(ps[6 + g][:, :], lhsT=onesb[:], rhs=s[:], start=(co == 0), stop=(co == nco - 1)),
                     reads=[bs, Bc], writes=[psb[6 + g]])
            return late
        return cons

    def post_finalize(stk, name, c0, vidx, gated):
        rs = sbt(stk, name + "_prs", [128, SG], F32)
        brs = Buf("prs")
        tr = Rot(K2(stk), name + "_ptmp", [128, 512], F32, 2)
        for g in range(NGRP):
            rstd_from_ssq(ps[6 + g][:, :], psb[6 + g], rs, brs, slice(g * 512, (g + 1) * 512), 1.0 / D, tr)
        mr = Rot(K2(stk), name + "_fm", [128, 512], F32, 4)
        hr = Rot(K2(stk), name + "_fh", [128, 512], F32, 4)
        gr = Rot(K2(stk), name + "_fg", [128, 512], F32, 4) if gated else None
        blocks = [(k, g) for k in range(16) for g in range(NGRP)]
        loaded = {}

        def load(n):
            k, g = blocks[n]
            cs = slice(g * 512, (g + 1) * 512)
            hs = slice(c0 + g * 512, c0 + (g + 1) * 512)
            m, bm = mr.next()
            h, bh = hr.next()
            dma("sp", m[:], D_mix[k, :, cs], [B_mix[k]], [bm])
            dma("sp", h[:], D_hT[k, :, hs], [B_hT[k]], [bh])
            gt = bg = None
            if gated:
                gt, bg = gr.next()
                dma("sp", gt[:], D_g[k, :, cs], [B_g[k]], [bg])
            loaded[n] = (m, bm, h, bh, gt, bg)

        PF = 2
        for n in range(min(PF, len(blocks))):
            load(n)
        for n in range(len(blocks)):
            if n + PF < len(blocks):
                load(n + PF)
            k, g = blocks[n]
            cs = slice(g * 512, (g + 1) * 512)
            hs = slice(c0 + g * 512, c0 + (g + 1) * 512)
            m, bm, h, bh, gt, bg = loaded.pop(n)
            P.op("dve", lambda e, m=m, k=k, cs=cs: e.scalar_tensor_tensor(
                out=m[:], in0=m[:], scalar=vecs[:, vidx, k:k + 1], in1=rs[:, cs], op0=ALU.mult, op1=ALU.mult),
                reads=[bm, brs, Bc], writes=[bm])
            if gated:
                P.op("dve", lambda e, m=m, gt=gt: e.tensor_tensor(out=m[:], in0=m[:], in1=gt[:], op=ALU.mult),
                     reads=[bm, bg], writes=[bm])
            P.op("dve", lambda e, m=m, h=h: e.tensor_tensor(out=h[:], in0=h[:], in1=m[:], op=ALU.add),
                 reads=[bm, bh], writes=[bh])
            dma("sp", D_hT[k, :, hs], h[:], [bh], [B_hT[k]])

    def load_xT(stk, name, Dsrc, bsrc, KC, c0=0):
        xT = sbt(stk, name, [128, KC, SG], BF16)
        bx = Buf(name)
        for k0 in range(0, KC, 8):
            dma("sp", xT[:, k0:k0 + 8, :], Dsrc[k0:k0 + 8, :, c0:c0 + SG].rearrange("k p t -> p k t"), [bsrc], [bx])
        return xT, bx

    def phase_input(s, src=None, Ddst=None, Bdst=None):
        c0 = s * SG
        if src is None:
            src, Ddst, Bdst = x_in, D_hT, B_hT
        with scope() as loc:
            xr = Rot(K2(loc), "in_x", [128, D], F32, 3)
            tr = Rot(K2(loc), "in_t", [128, 16, 128], F32, 2)
            pre = {}

            def load(i):
                xt, bx = xr.next()
                dma("sp", xt[:], src[c0 + i * 128:c0 + (i + 1) * 128, :], [], [bx])
                pre[i] = (xt, bx)

            load(0)
            for i in range(NT):
                if i + 1 < NT:
                    load(i + 1)
                xt, bx = pre.pop(i)
                tt, bt = tr.next()
                for q in range(4):
                    pt, pb = bank()
                    for j in range(4):
                        k = q * 4 + j
                        P.op("pe", lambda e, k=k, j=j, pt=pt, xt=xt: e.transpose(
                            out=pt[:, j * 128:(j + 1) * 128], in_=xt[:, k * 128:(k + 1) * 128], identity=identf[:]),
                            reads=[bx, Bc], writes=[pb])
                    P.op("act" if q % 2 else "dve",
                         (lambda e, q=q, pt=pt, tt=tt: e.copy(out=tt[:, q * 4:(q + 1) * 4, :].rearrange("p a b -> p (a b)"), in_=pt[:, :])) if q % 2 else
                         (lambda e, q=q, pt=pt, tt=tt: e.tensor_copy(out=tt[:, q * 4:(q + 1) * 4, :].rearrange("p a b -> p (a b)"), in_=pt[:, :])),
                         reads=[pb], writes=[bt])
                dma("sp", Ddst[:, :, c0 + i * 128:c0 + (i + 1) * 128].rearrange("k p t -> p k t"), tt[:], [bt], Bdst)

    def phase_output(s):
        c0 = s * SG
        with scope() as loc:
            hr = Rot(K2(loc), "o_h", [128, 16, 128], F32, 3)
            orr = Rot(K2(loc), "o_o", [128, D], F32, 2)
            pre = {}

            def load(i):
                ht, bh = hr.next()
                dma("sp", ht[:], D_hT[:, :, c0 + i * 128:c0 + (i + 1) * 128].rearrange("k p t -> p k t"), B_hT, [bh])
                pre[i] = (ht, bh)

            load(0)
            for i in range(NT):
                if i + 1 < NT:
                    load(i + 1)
                ht, bh = pre.pop(i)
                ot, bo = orr.next()
                for q in range(4):
                    pt, pb = bank()
                    for j in range(4):
                        k = q * 4 + j
                        P.op("pe", lambda e, k=k, j=j, pt=pt, ht=ht: e.transpose(
                            out=pt[:, j * 128:(j + 1) * 128], in_=ht[:, k, :], identity=identf[:]),
                            reads=[bh, Bc], writes=[pb])
                    P.op("act" if q % 2 else "dve",
                         (lambda e, q=q, pt=pt, ot=ot: e.copy(out=ot[:, q * 512:(q + 1) * 512], in_=pt[:, :])) if q % 2 else
                         (lambda e, q=q, pt=pt, ot=ot: e.tensor_copy(out=ot[:, q * 512:(q + 1) * 512], in_=pt[:, :])),
                         reads=[pb], writes=[bo])
                dma("sp", out[c0 + i * 128:c0 + (i + 1) * 128, :], ot[:], [bo], [B_out])

    def phase_ssd_inproj(s, prefix=False):
        c0 = s * SG
        with scope() as loc:
            if prefix:
                hnT, b_hn = norm_fm(loc, 0, 0, "n0p", Dsrc=D_hTp, Bsrc=B_hTp)
            else:
                hnT, b_hn = norm_fm(loc, c0, 0, "n0")
            szr = Rot(K2(loc), "b0_sz", [128, 512], BF16, 3)
            xpr = Rot(K2(loc), "b0_xp", [128, 515], F32, 3)
            acr = Rot(K2(loc), "b0_ac", [128, 512], F32, 3)
            xbr = Rot(K2(loc), "b0_xb", [128, 512], BF16, 5)
            str_ = Rot(K2(loc), "b0_st", [128, 4, 128], BF16, 3)
            dtr = Rot(K2(loc), "b0_dt", [64, 512], F32, 2)

            def cons(co, g, pts, pbs, rows):
                pt, pb = pts[0], pbs[0]
                cs = slice(g * 512, (g + 1) * 512)
                if co < 32:
                    t, bt = szr.next()
                    P.op("act", lambda e: e.activation(out=t[:], in_=pt[:, :], func=AF.Silu), reads=[pb], writes=[bt])
                    dma("sp", D_sz[co, :, cs], t[:], [bt], [B_sz])
                elif co < 80:
                    cc = co - 32
                    xp, bxp = xpr.next()
                    P.op("act", lambda e: e.copy(out=xp[:, 0:3], in_=halo[:, cc, :]), reads=[B_halo], writes=[bxp])
                    P.op("act", lambda e: e.copy(out=xp[:, 3:515], in_=pt[:, :]), reads=[pb], writes=[bxp])
                    P.op("dve", lambda e: e.tensor_copy(out=halo[:, cc, :], in_=xp[:, 512:515]), reads=[bxp], writes=[B_halo])
                    if prefix and cc >= 40:
                        return
                    ac, bac = acr.next()
                    P.op("dve", lambda e: e.tensor_scalar(out=ac[:], in0=xp[:, 0:512], scalar1=cw_s[:, cc, 0:1],
                                                          scalar2=cb_s[:, cc:cc + 1], op0=ALU.mult, op1=ALU.add),
                         reads=[bxp, Bc], writes=[bac])
                    for j in (1, 2, 3):
                        P.op("dve", lambda e, j=j: e.scalar_tensor_tensor(out=ac[:], in0=xp[:, j:j + 512], scalar=cw_s[:, cc, j:j + 1],
                                                                          in1=ac[:], op0=ALU.mult, op1=ALU.add),
                             reads=[bxp, bac, Bc], writes=[bac])
                    xb, bxb = xbr.next()
                    P.op("act", lambda e: e.activation(out=xb[:], in_=ac[:], func=AF.Silu), reads=[bac], writes=[bxb])
                    late = None
                    if cc < 40:
                        dma("sp", D_xbT[cc, :, cs], xb[:], [bxb], [B_xbT])
                    if 32 <= cc < 40 and not prefix:
                        dma("sp", D_BT[cc - 32, :, cs], xb[:], [bxb], [B_BT])
                    elif cc >= 40:
                        dma("sp", D_CT[cc - 40, :, cs], xb[:], [bxb], [B_CT])
                    return late
                else:
                    t, bt = dtr.next()
                    P.op("act", lambda e: e.copy(out=t[:], in_=pt[0:64, :]), reads=[pb], writes=[bt])
                    dma("sp", D_dtT[:, cs], t[:], [bt], [B_dtT])

            if prefix:
                linear_fm(loc, "b0p", [ssd_w_in], 16, 4096, 6144, hnT, b_hn, cons, CW=512)
                linear_fm(loc, "b0q", [ssd_w_in], 16, 10240, 64, hnT, b_hn, cons, CW=64, nbuf=1)
                P.op("dve", lambda e: e.tensor_scalar(out=halo[:], in0=halo[:], scalar1=role_s[:, 0:1], scalar2=None, op0=ALU.mult),
                     reads=[B_halo, Bc], writes=[B_halo])
            else:
                linear_fm(loc, "b0", [ssd_w_in], 16, 0, SSD_IN, hnT, b_hn, cons, CW=512)
            xinr = Rot(K2(loc), "b0_xin", [128, 40, 128], BF16, 2)
            stgr = Rot(K2(loc), "b0_stg", [128, 40 * 128], BF16, 2)
            pre = {}

            def tload(i):
                xin, bxin = xinr.next()
                for c8 in range(0, 40, 8):
                    dma("sp", xin[:, c8:c8 + 8, :], D_xbT[c8:c8 + 8, :, i * 128:(i + 1) * 128].rearrange("c p t -> p c t"), [B_xbT], [bxin])
                pre[i] = (xin, bxin)

            tload(0)
            for i in range(NT):
                if i + 1 < NT:
                    tload(i + 1)
                xin, bxin = pre.pop(i)
                stg, bstg = stgr.next()
                for q in range(10):
                    pt2, pb2 = bank(0, 6)
                    pv = pt2[:, :].bitcast(BF16)
                    for j in range(4):
                        P.op("pe", lambda e, pv=pv, j=j, q=q, xin=xin: e.transpose(out=pv[:, j * 128:(j + 1) * 128], in_=xin[:, q * 4 + j, :], identity=ident[:]),
                             reads=[bxin, Bc], writes=[pb2])
                    if q % 2:
                        P.op("act", lambda e, pv=pv, q=q, stg=stg: e.copy(out=stg[:, q * 512:(q + 1) * 512], in_=pv[:, 0:512]), reads=[pb2], writes=[bstg])
                    else:
                        P.op("dve", lambda e, pv=pv, q=q, stg=stg: e.tensor_copy(out=stg[:, q * 512:(q + 1) * 512], in_=pv[:, 0:512]), reads=[pb2], writes=[bstg])
                dma("sp", D_xs[i, :, :], stg[:, 0:DI], [bstg], [B_xs])
                dma("sp", D_Btm[i, :, :], stg[:, DI:DI + NG * DS], [bstg], [B_Btm])

    def phase_ssd_core(s, prefix=False, load_prev=None):
        if load_prev is None:
            load_prev = s > 0
        with scope() as loc:
            L = K2(loc)
            xsr = Rot(L, "c0_xs", [128, DI], BF16, 2)
            btmr = Rot(L, "c0_btm", [128, NG * DS], BF16, 2)
            btr = Rot(L, "c0_bt", [128, NG, 128], BF16, 2)
            ctr = Rot(L, "c0_ct", [128, NG, 128], BF16, 2)
            dtTr = Rot(L, "c0_dtT", [64, 128], F32, 2)
            szr = Rot(L, "c0_sz", [128, 32, 128], BF16, 2)
            sm = Rot(L, "c0_sm", [128, 10, 64], F32, 2)
            acTr = Rot(L, "c0_acT", [64, 128], F32, 2)
            achr = Rot(L, "c0_ach", [64, 2, 128], BF16, 2)
            xdtr = Rot(L, "c0_xdt", [128, 64, 64], BF16, 1)
            xddr = Rot(L, "c0_xdd", [128, 64, 64], BF16, 1)
            cbr = Rot(L, "c0_cb", [128, 128], F32, 2)
            decr = Rot(L, "c0_dec", [128, 8, 128], F32, 2)
            mtr = Rot(L, "c0_mt", [128, 8, 128], BF16, 2)
            t1r = Rot(L, "c0_t1", [128, 8, 64], F32, 2)
            t2r = Rot(L, "c0_t2", [128, 8, 64], F32, 2)
            yr = Rot(L, "c0_y", [128, DI], F32, 1)
            yzr = Rot(L, "c0_yz", [128, 32, 128], F32, 1)
            sqr = Rot(L, "c0_sq", [128, 4, 128], BF16, 2)
            rsr = Rot(L, "c0_rs", [128, 1024], F32, 1)
            rtr = Rot(L, "c0_rt", [128, 1024], F32, 1)
            ynr = Rot(L, "c0_yn", [128, 32, 128], BF16, 1)
            prev = sbt(loc, "c0_prev", [128, NG, 512], F32)
            prevb = sbt(loc, "c0_prevb", [128, NG, 512], BF16)
            B_prev = [Buf("prev%d" % g) for g in range(NG)]
            B_prevb = [Buf("prevb%d" % g) for g in range(NG)]
            if not load_prev:
                P.op("dve", lambda e: e.memset(prev[:], 0.0), writes=B_prev)
            else:
                dma("sp", prev[:].rearrange("p a b -> p (a b)"), D_prev, [B_Dprev], B_prev)
            if not prefix:
                for g in range(NG):
                    P.op("act", lambda e, g=g: e.copy(out=prevb[:, g, :], in_=prev[:, g, :]), reads=[B_prev[g]], writes=[B_prevb[g]])
            pre = {}

            def loads(i):
                ts_ = slice(i * 128, (i + 1) * 128)
                xs, bxs = xsr.next()
                btm, bbtm = btmr.next()
                bt, bbt = btr.next()
                ct, bct = ctr.next()
                dtT, bdtT = dtTr.next()
                sz, bsz = szr.next()
                dma("sp", xs[:], D_xs[i, :, :], [B_xs], [bxs])
                dma("sp", btm[:], D_Btm[i, :, :], [B_Btm], [bbtm])
                dma("sp", dtT[:], D_dtT[:, ts_], [B_dtT], [bdtT])
                if not prefix:
                    dma("sp", bt[:], D_BT[:, :, ts_].rearrange("g p t -> p g t"), [B_BT], [bbt])
                    dma("sp", ct[:], D_CT[:, :, ts_].rearrange("g p t -> p g t"), [B_CT], [bct])
                    for c8 in range(0, 32, 8):
                        dma("sp", sz[:, c8:c8 + 8, :], D_sz[c8:c8 + 8, :, ts_].rearrange("c p t -> p c t"), [B_sz], [bsz])
                pre[i] = (xs, bxs, btm, bbtm, bt, bbt, ct, bct, dtT, bdtT, sz, bsz)

            loads(0)
            for i in range(NT):
                ts_ = slice(i * 128, (i + 1) * 128)
                if i + 1 < NT:
                    loads(i + 1)
                xs, bxs, btm, bbtm, bt, bbt, ct, bct, dtT, bdtT, sz, bsz = pre.pop(i)
                m, bm = sm.next()
                pt, pb = bank(0, 6)
                P.op("pe", lambda e, pt=pt, dtT=dtT: e.transpose(out=pt[:, 0:64], in_=dtT[:, :], identity=identf[0:64, 0:64]),
                     reads=[bdtT, Bc], writes=[pb])
                P.op("dve", lambda e, pt=pt, m=m: e.tensor_tensor(out=m[:, 0, :], in0=pt[:, 0:64], in1=r64[:, 0, :], op=ALU.add),
                     reads=[pb, Bc], writes=[bm])
                P.op("act", lambda e, m=m: e.activation(out=m[:, 1, :], in_=m[:, 0, :], func=AF.Exp), reads=[bm], writes=[bm])
                P.op("act", lambda e, m=m: e.activation(out=m[:, 2, :], in_=m[:, 1, :], func=AF.Ln, bias=1.0), reads=[bm], writes=[bm])
                P.op("dve", lambda e, m=m: e.tensor_tensor(out=m[:, 3, :], in0=m[:, 2, :], in1=arep[:], op=ALU.mult),
                     reads=[bm, Bc], writes=[bm])
                pt2, pb2 = bank(0, 6)
                P.op("pe", lambda e, pt2=pt2, m=m: e.matmul(pt2[:, 0:64], lhsT=triuf[:], rhs=m[:, 3, :], start=True, stop=True),
                     reads=[bm, Bc], writes=[pb2])
                P.op("pe", lambda e, pt2=pt2, m=m: e.matmul(pt2[:, 64:128], lhsT=onesf[:], rhs=m[:, 3, :], start=True, stop=True),
                     reads=[bm, Bc], writes=[pb2])
                P.op("dve", lambda e, pt2=pt2, m=m: e.tensor_copy(out=m[:, 4, :], in_=pt2[:, 0:64]), reads=[pb2], writes=[bm])
                P.op("dve", lambda e, pt2=pt2, m=m: e.tensor_scalar(out=m[:, 5, :], in0=pt2[:, 0:64], scalar1=-1.0, scalar2=None, op0=ALU.mult),
                     reads=[pb2], writes=[bm])
                P.op("act", lambda e, pt2=pt2, m=m: e.activation(out=m[:, 6, :], in_=pt2[:, 0:64], func=AF.Exp), reads=[pb2], writes=[bm])
                P.op("act", lambda e, pt2=pt2, m=m: e.activation(out=m[:, 7, :], in_=pt2[:, 64:128], func=AF.Exp), reads=[pb2], writes=[bm])
                P.op("dve", lambda e, pt2=pt2, m=m: e.tensor_tensor(out=m[:, 8, :], in0=pt2[:, 64:128], in1=m[:, 4, :], op=ALU.subtract),
                     reads=[pb2, bm], writes=[bm])
                P.op("act", lambda e, m=m: e.activation(out=m[:, 8, :], in_=m[:, 8, :], func=AF.Exp), reads=[bm], writes=[bm])
                P.op("dve", lambda e, m=m: e.tensor_tensor(out=m[:, 9, :], in0=m[:, 2, :], in1=m[:, 8, :], op=ALU.mult),
                     reads=[bm], writes=[bm])
                xs3 = xs[:].rearrange("p (h d) -> p h d", h=64)
                xdd, bxdd = xddr.next()
                P.op("dve", lambda e, xs3=xs3, xdd=xdd, m=m: e.tensor_tensor(
                    out=xdd[:], in0=xs3, in1=m[:, 9, :].unsqueeze(2).to_broadcast([128, 64, 64]), op=ALU.mult),
                    reads=[bxs, bm], writes=[bxdd])
                if prefix:
                    for g in range(NG):
                        hs = slice(g * 8, (g + 1) * 8)
                        pSt, pbSt = bank(0, 6)
                        P.op("pe", lambda e, pSt=pSt, btm=btm, xdd=xdd, g=g, hs=hs: e.matmul(
                            pSt[:, :], lhsT=btm[:, g * 128:(g + 1) * 128], rhs=xdd[:, hs, :].rearrange("p a b -> p (a b)"), start=True, stop=True),
                            reads=[bbtm, bxdd], writes=[pbSt])
                        pv3 = prev[:, g, :].rearrange("p (a b) -> p a b", a=8)
                        P.op("dve", lambda e, pv3=pv3, m=m, hs=hs: e.tensor_tensor(
                            out=pv3, in0=pv3, in1=m[:, 7, hs].unsqueeze(2).to_broadcast([128, 8, 64]), op=ALU.mult),
                            reads=[B_prev[g], bm], writes=[B_prev[g]])
                        P.op("dve", lambda e, pSt=pSt, g=g: e.tensor_tensor(out=prev[:, g, :], in0=prev[:, g, :], in1=pSt[:, :], op=ALU.add),
                             reads=[B_prev[g], pbSt], writes=[B_prev[g]])
                    continue
                pt3, pb3 = bank(0, 6)
                P.op("pe", lambda e, pt3=pt3, m=m: e.transpose(out=pt3[0:64, 0:128], in_=m[:, 4, :], identity=identf[:]),
                     reads=[bm, Bc], writes=[pb3])
                acT, bacT = acTr.next()
                ach, bach = achr.next()
                P.op("dve", lambda e, pt3=pt3, acT=acT: e.tensor_copy(out=acT[:], in_=pt3[0:64, 0:128]), reads=[pb3], writes=[bacT])
                P.op("dve", lambda e, acT=acT, ach=ach: e.tensor_copy(out=ach[:, 0, :], in_=acT[:]), reads=[bacT], writes=[bach])
                P.op("dve", lambda e, acT=acT, ach=ach: e.tensor_tensor(out=acT[:], in0=acT[:], in1=ach[:, 0, :], op=ALU.subtract),
                     reads=[bacT, bach], writes=[bacT])
                P.op("dve", lambda e, acT=acT, ach=ach: e.tensor_copy(out=ach[:, 1, :], in_=acT[:]), reads=[bacT], writes=[bach])
                xdt, bxdt = xdtr.next()
                P.op("dve", lambda e, xs3=xs3, xdt=xdt, m=m: e.tensor_tensor(
                    out=xdt[:], in0=xs3, in1=m[:, 2, :].unsqueeze(2).to_broadcast([128, 64, 64]), op=ALU.mult),
                    reads=[bxs, bm], writes=[bxdt])
                y, by = yr.next()
                for g in range(NG):
                    hs = slice(g * 8, (g + 1) * 8)
                    pc, pbc = bank(0, 6)
                    P.op("pe", lambda e, pc=pc, bt=bt, ct=ct, g=g: e.matmul(pc[:, 0:128], lhsT=bt[:, g, :], rhs=ct[:, g, :], start=True, stop=True),
                         reads=[bbt, bct], writes=[pbc])
                    cb, bcb = cbr.next()
                    P.op("dve", lambda e, pc=pc, cb=cb: e.tensor_copy(out=cb[:], in_=pc[:, 0:128]), reads=[pbc], writes=[bcb])
                    dec, bdec = decr.next()
                    mt, bmt = mtr.next()
                    for half in range(2):
                        pS, pbS = bank(0, 6)
                        for h4 in range(4):
                            h = half * 4 + h4
                            hh = g * 8 + h
                            osl = slice(h4 * 128, (h4 + 1) * 128)
                            P.op("pe", lambda e, pS=pS, osl=osl: e.matmul(pS[:, osl], lhsT=ident[:], rhs=negm[:], start=True, stop=False),
                                 reads=[Bc], writes=[pbS])
                            P.op("pe", lambda e, pS=pS, osl=osl, hh=hh, ach=ach: e.matmul(pS[:, osl], lhsT=selS[:, hh, :], rhs=ach[:, 0, :], start=False, stop=False),
                                 reads=[Bc, bach], writes=[pbS])
                            P.op("pe", lambda e, pS=pS, osl=osl, hh=hh, ach=ach: e.matmul(pS[:, osl], lhsT=selS[:, hh, :], rhs=ach[:, 1, :], start=False, stop=True),
                                 reads=[Bc, bach], writes=[pbS])
                        for h4 in range(4):
                            h = half * 4 + h4
                            hh = g * 8 + h
                            P.op("act", lambda e, pS=pS, h4=h4, h=h, hh=hh, dec=dec, m=m: e.activation(
                                out=dec[:, h, :], in_=pS[:, h4 * 128:(h4 + 1) * 128], func=AF.Exp, bias=m[:, 5, hh:hh + 1]),
                                reads=[pbS, bm], writes=[bdec])
                    P.op("dve", lambda e, mt=mt, dec=dec, cb=cb: e.tensor_tensor(
                        out=mt[:], in0=dec[:], in1=cb[:].unsqueeze(1).to_broadcast([128, 8, 128]), op=ALU.mult),
                        reads=[bdec, bcb], writes=[bmt])
                    pY, pbY = bank(0, 6)
                    for h in range(8):
                        hh = g * 8 + h
                        P.op("pe", lambda e, pY=pY, h=h, hh=hh, mt=mt, xdt=xdt: e.matmul(
                            pY[:, h * 64:(h + 1) * 64], lhsT=mt[:, h, :], rhs=xdt[:, hh, :], start=True, stop=True),
                            reads=[bmt, bxdt], writes=[pbY])
                    pO, pbO = bank(0, 6)
                    P.op("pe", lambda e, pO=pO, ct=ct, g=g: e.matmul(pO[:, :], lhsT=ct[:, g, :], rhs=prevb[:, g, :], start=True, stop=True),
                         reads=[bct, B_prevb[g]], writes=[pbO])
                    pSt, pbSt = bank(0, 6)
                    P.op("pe", lambda e, pSt=pSt, btm=btm, xdd=xdd, g=g, hs=hs: e.matmul(
                        pSt[:, :], lhsT=btm[:, g * 128:(g + 1) * 128], rhs=xdd[:, hs, :].rearrange("p a b -> p (a b)"), start=True, stop=True),
                        reads=[bbtm, bxdd], writes=[pbSt])
                    t1, bt1 = t1r.next()
                    t2, bt2 = t2r.next()
                    P.op("dve", lambda e, t1=t1, pO=pO, m=m, hs=hs: e.tensor_tensor(
                        out=t1[:], in0=pO[:, :].rearrange("p (a b) -> p a b", a=8),
                        in1=m[:, 6, hs].unsqueeze(2).to_broadcast([128, 8, 64]), op=ALU.mult),
                        reads=[pbO, bm], writes=[bt1])
                    P.op("dve", lambda e, t1=t1, pY=pY: e.tensor_tensor(
                        out=t1[:], in0=t1[:], in1=pY[:, :].rearrange("p (a b) -> p a b", a=8), op=ALU.add),
                        reads=[pbY, bt1], writes=[bt1])
                    P.op("dve", lambda e, t2=t2, xs3=xs3, hs=hs: e.tensor_tensor(
                        out=t2[:], in0=xs3[:, hs, :], in1=r64[:, 2, hs].unsqueeze(2).to_broadcast([128, 8, 64]), op=ALU.mult),
                        reads=[bxs, Bc], writes=[bt2])
                    P.op("dve", lambda e, t1=t1, t2=t2, y=y, g=g: e.tensor_tensor(
                        out=y[:, g * 512:(g + 1) * 512].rearrange("p (a b) -> p a b", a=8), in0=t1[:], in1=t2[:], op=ALU.add),
                        reads=[bt1, bt2], writes=[by])
                    pv3 = prev[:, g, :].rearrange("p (a b) -> p a b", a=8)
                    P.op("dve", lambda e, pv3=pv3, m=m, hs=hs: e.tensor_tensor(
                        out=pv3, in0=pv3, in1=m[:, 7, hs].unsqueeze(2).to_broadcast([128, 8, 64]), op=ALU.mult),
                        reads=[B_prev[g], bm], writes=[B_prev[g]])
                    P.op("dve", lambda e, pSt=pSt, g=g: e.tensor_tensor(out=prev[:, g, :], in0=prev[:, g, :], in1=pSt[:, :], op=ALU.add),
                         reads=[B_prev[g], pbSt], writes=[B_prev[g]])
                    P.op("act", lambda e, g=g: e.copy(out=prevb[:, g, :], in_=prev[:, g, :]), reads=[B_prev[g]], writes=[B_prevb[g]])
                yz, byz = yzr.next()
                rs, brs = rsr.next()
                rt, brt = rtr.next()
                yn, byn = ynr.next()
                for mg in range(8):
                    pT, pbT = bank(0, 4)
                    for j in range(4):
                        c = mg * 4 + j
                        P.op("pe", lambda e, pT=pT, j=j, c=c, y=y: e.transpose(
                            out=pT[:, j * 128:(j + 1) * 128], in_=y[:, c * 128:(c + 1) * 128], identity=identf[:]),
                            reads=[by, Bc], writes=[pbT])
                    P.op("dve", lambda e, pT=pT, mg=mg, yz=yz, sz=sz: e.tensor_tensor(
                        out=yz[:, mg * 4:(mg + 1) * 4, :], in0=pT[:, :].rearrange("p (a b) -> p a b", a=4),
                        in1=sz[:, mg * 4:(mg + 1) * 4, :], op=ALU.mult), reads=[pbT, bsz], writes=[byz])
                    sq, bsq = sqr.next()
                    P.op("act", lambda e, sq=sq, yz=yz, mg=mg: e.activation(out=sq[:], in_=yz[:, mg * 4:(mg + 1) * 4, :], func=AF.Square),
                         reads=[byz], writes=[bsq])
                    pq = ps[4 + mg // 4]
                    pbq = psb[4 + mg // 4]
                    for j in range(4):
                        P.op("pe", lambda e, pq=pq, mg=mg, j=j, sq=sq: e.matmul(
                            pq[:, (mg % 4) * 128:(mg % 4 + 1) * 128], lhsT=onesb[:], rhs=sq[:, j, :], start=(j == 0), stop=(j == 3)),
                            reads=[bsq, Bc], writes=[pbq])
                for hb_ in range(2):
                    P.op("act", lambda e, hb_=hb_, rt=rt: e.activation(out=rt[:, hb_ * 512:(hb_ + 1) * 512], in_=ps[4 + hb_][:, :], func=AF.Sqrt,
                                                                         bias=epsb[:, 0:1], scale=1.0 / 512), reads=[psb[4 + hb_], Bc], writes=[brt])
                P.op("dve", lambda e, rs=rs, rt=rt: e.reciprocal(out=rs[:], in_=rt[:]), reads=[brt], writes=[brs])
                for c in range(32):
                    mg = c // 4
                    P.op("dve", lambda e, c=c, mg=mg, yn=yn, yz=yz, rs=rs: e.scalar_tensor_tensor(
                        out=yn[:, c, :], in0=yz[:, c, :], scalar=nw_s[:, c:c + 1], in1=rs[:, mg * 128:(mg + 1) * 128],
                        op0=ALU.mult, op1=ALU.mult), reads=[byz, brs, Bc], writes=[byn])
                for c8 in range(0, 32, 8):
                    dma("sp", D_yn[c8:c8 + 8, :, ts_].rearrange("c p t -> p c t"), yn[:, c8:c8 + 8, :], [byn], [B_yn])
            if prefix:
                for g in range(NG):
                    P.op("dve", lambda e, g=g: e.tensor_scalar(out=prev[:, g, :], in0=prev[:, g, :], scalar1=role_s[:, 0:1], scalar2=None, op0=ALU.mult),
                         reads=[B_prev[g], Bc], writes=[B_prev[g]])
            dma("sp", D_prev, prev[:].rearrange("p a b -> p (a b)"), B_prev, [B_Dprev])

    def phase_proj_post(s, name, Dsrc, bsrc, KC, W, vidx, CW):
        c0 = s * SG
        with scope() as loc:
            xT, bx = load_xT(loc, name + "_x", Dsrc, bsrc, KC)
            cons = post_consumer(loc, name, None)
            linear_fm(loc, name, [W], KC, 0, D, xT, bx, cons, CW=CW)
        with scope() as loc:
            post_finalize(loc, name, c0, vidx, False)

    def phase_ffn(s, layer):
        c0 = s * SG
        with scope() as loc:
            actT = sbt(loc, "f_act", [128, 44, SG], BF16)
            bact = Buf("actT")
            with scope() as l2:
                hnT, b_hn = norm_fm(l2, c0, 4 + layer, "nf", nhb=1)
                sgr = Rot(K2(l2), "f_sg", [128, 512], F32, 3)

                def cons(co, g, pts, pbs, rows):
                    t, bt = sgr.next()
                    P.op("act", lambda e: e.activation(out=t[:], in_=pts[0][:, :], func=AF.Silu), reads=[pbs[0]], writes=[bt])
                    P.op("dve", lambda e: e.tensor_tensor(out=actT[:, co, g * 512:(g + 1) * 512], in0=t[:], in1=pts[1][:, :], op=ALU.mult),
                         reads=[bt, pbs[1]], writes=[bact])

                linear_fm(l2, "fgu", [ffn_wg[layer], ffn_wu[layer]], 16, 0, FFN, hnT, b_hn, cons, CW=256, nbuf=2)
            with scope() as l2:
                cons2 = post_consumer(l2, "fd", None)
                linear_fm(l2, "fd", [ffn_wd[layer]], 44, 0, D, actT, bact, cons2, CW=128)
        with scope() as loc:
            post_finalize(loc, "fd", c0, 6 + layer, False)

    def phase_ple(s, layer):
        c0 = s * SG
        with scope() as loc:
            L = K2(loc)
            pT = sbt(loc, "pl_pT", [128, 2, SG], BF16)
            bpT = Buf("pT")
            pr = Rot(L, "pl_p", [128, PLE], F32, 2)
            for i in range(NT):
                pt_, bp = pr.next()
                dma("sp", pt_[:], p_in[layer, c0 + i * 128:c0 + (i + 1) * 128, :], [], [bp])
                pt, pb = bank(0, 6)
                for j in range(2):
                    P.op("pe", lambda e, pt=pt, pt_=pt_, j=j: e.transpose(out=pt[:, j * 128:(j + 1) * 128], in_=pt_[:, j * 128:(j + 1) * 128], identity=identf[:]),
                         reads=[bp, Bc], writes=[pb])
                P.op("dve", lambda e, pt=pt, i=i: e.tensor_copy(out=pT[:, :, i * 128:(i + 1) * 128], in_=pt[:, 0:256].rearrange("p (a b) -> p a b", a=2)),
                     reads=[pb], writes=[bpT])
            cons = post_consumer(loc, "plp", None)
            linear_fm(loc, "plp", [ple_wp[layer]], 2, 0, D, pT, bpT, cons, CW=512)
            hbT = sbt(loc, "pl_hb", [128, 16, SG], BF16)
            bhb = Buf("hbT")
            hr = Rot(L, "pl_h", [128, 512], F32, 3)
            for k in range(16):
                for g in range(NGRP):
                    h, bh = hr.next()
                    dma("sp", h[:], D_hT[k, :, c0 + g * 512:c0 + (g + 1) * 512], [B_hT[k]], [bh])
                    P.op("act", lambda e, h=h, k=k, g=g: e.copy(out=hbT[:, k, g * 512:(g + 1) * 512], in_=h[:]), reads=[bh], writes=[bhb])
            gr = Rot(L, "pl_g", [128, 512], F32, 3)

            def consg(co, g, pts, pbs, rows):
                t, bt = gr.next()
                P.op("act", lambda e: e.activation(out=t[:], in_=pts[0][:, :], func=AF.Sigmoid), reads=[pbs[0]], writes=[bt])
                dma("sp", D_g[co, :, g * 512:(g + 1) * 512], t[:], [bt], [B_g[co]])

            linear_fm(loc, "plg", [ple_wg[layer]], 16, 0, D, hbT, bhb, consg, CW=512)
        with scope() as loc:
            post_finalize(loc, "pl", c0, 8 + layer, True)

    def phase_fox_inproj(s):
        c0 = s * SG
        with scope() as loc:
            L = K2(loc)
            hnT, b_hn = norm_fm(loc, c0, 1, "n1")
            qr = Rot(L, "b1_q", [128, 512], BF16, 3)
            vr = Rot(L, "b1_v", [128, 512], BF16, 5)
            str_ = Rot(L, "b1_st", [128, 4, 128], BF16, 3)
            fT = sbt(loc, "b1_fT", [16, SG], F32)
            bfT = Buf("fT")

            def cons(co, g, pts, pbs, rows):
                pt, pb = pts[0], pbs[0]
                cs = slice(g * 512, (g + 1) * 512)
                if co < 16:
                    t, bt = qr.next()
                    P.op("act", lambda e: e.activation(out=t[:], in_=pt[:, :], func=AF.Copy, scale=float(FD) ** -0.5), reads=[pb], writes=[bt])
                    dma("sp", D_QT[co, :, cs], t[:], [bt], [B_QT])
                elif co < 32:
                    t, bt = qr.next()
                    P.op("act", lambda e: e.copy(out=t[:], in_=pt[:, :]), reads=[pb], writes=[bt])
                    if paired:
                        hk = co - 16
                        dma("sp", X_K[hk // 4][(hk % 4) * 128:(hk % 4 + 1) * 128, cs], t[:], [bt], [B_XK[hk // 4]])
                        if hk % 4 == 3 and g == NGRP - 1:
                            j = hk // 4
                            P.op("pool", lambda e, j=j: e.collective_compute("AllGather", ALU.bypass, replica_groups=PAIRS,
                                                                             ins=[X_K[j]], outs=[G_K[j]]),
                                 reads=[B_XK[j]], writes=[B_GK[j]], cc=True)
                    else:
                        dma("sp", D_KT[co - 16, :, c0 + g * 512:c0 + (g + 1) * 512], t[:], [bt], [B_KT])
                elif co < 48:
                    h = co - 32
                    t, bt = vr.next()
                    P.op("act", lambda e: e.copy(out=t[:], in_=pt[:, :]), reads=[pb], writes=[bt])
                    def late():
                        pt2, pb2 = bank(0, 6)
                        pv = pt2[:, :].bitcast(BF16)
                        for j in range(4):
                            P.op("pe", lambda e, j=j: e.transpose(out=pv[:, j * 128:(j + 1) * 128], in_=t[:, j * 128:(j + 1) * 128], identity=ident[:]),
                                 reads=[bt, Bc], writes=[pb2])
                        sg_, bsg = str_.next()
                        P.op("dve", lambda e: e.tensor_copy(out=sg_[:].rearrange("p a b -> p (a b)"), in_=pv[:, 0:512]), reads=[pb2], writes=[bsg])
                        if paired:
                            for jj in range(2):
                                j = 2 * g + jj
                                dma("sp", X_V[j][:, h * 128:(h + 1) * 128].rearrange("(t p) c -> p t c", p=128), sg_[:, 2 * jj:2 * jj + 2, :],
                                    [bsg], [B_XV[j]])
                                if h == FH - 1:
                                    P.op("pool", lambda e, j=j: e.collective_compute("AllGather", ALU.bypass, replica_groups=PAIRS,
                                                                                     ins=[X_V[j]], outs=[G_V[j]]),
                                         reads=[B_XV[j]], writes=[B_GV[j]], cc=True)
                        else:
                            t0 = c0 // 128 + g * 4
                            dma("sp", D_V[t0:t0 + 4, :, h * 128:(h + 1) * 128].rearrange("j p c -> p j c"), sg_[:], [bsg], [B_V])
                    late()
                    return None
                else:
                    P.op("act", lambda e: e.copy(out=fT[:, cs], in_=pt[0:16, :]), reads=[pb], writes=[bfT])

            linear_fm(loc, "b1", [fox_w_in], 16, 0, FOX_IN, hnT, b_hn, cons, CW=512)
            smr = Rot(L, "b1_sm", [128, 4, 16], F32, 2)
            csr = Rot(L, "b1_cs", [128, 16], F32, 2)
            cTr = Rot(L, "b1_cT", [16, 128], F32, 2)
            for i in range(NT):
                ti = (NT + i) if paired else (c0 // 128 + i)
                pt, pb = bank(0, 6)
                P.op("pe", lambda e, pt=pt, i=i: e.transpose(out=pt[:, 0:16], in_=fT[:, i * 128:(i + 1) * 128], identity=identf[0:16, 0:16]),
                     reads=[bfT, Bc], writes=[pb])
                m, bm = smr.next()
                P.op("dve", lambda e, pt=pt, m=m: e.tensor_tensor(out=m[:, 0, :], in0=pt[:, 0:16], in1=r16[:], op=ALU.add), reads=[pb, Bc], writes=[bm])
                P.op("act", lambda e, m=m: e.activation(out=m[:, 1, :], in_=m[:, 0, :], func=AF.Exp, scale=-1.0), reads=[bm], writes=[bm])
                P.op("act", lambda e, m=m: e.activation(out=m[:, 2, :], in_=m[:, 1, :], func=AF.Ln, bias=1.0), reads=[bm], writes=[bm])
                P.op("dve", lambda e, m=m: e.tensor_scalar(out=m[:, 3, :], in0=m[:, 2, :], scalar1=-1.0, scalar2=None, op0=ALU.mult), reads=[bm], writes=[bm])
                pt2, pb2 = bank(0, 6)
                P.op("pe", lambda e, pt2=pt2, m=m: e.matmul(pt2[:, 0:16], lhsT=triuf[:], rhs=m[:, 3, :], start=True, stop=True), reads=[bm, Bc], writes=[pb2])
                P.op("pe", lambda e, pt2=pt2, m=m: e.matmul(pt2[:, 16:32], lhsT=onesf[:], rhs=m[:, 3, :], start=True, stop=True), reads=[bm, Bc], writes=[pb2])
                cs_, bcs = csr.next()
                P.op("dve", lambda e, pt2=pt2, cs_=cs_: e.tensor_tensor(out=cs_[:], in0=pt2[:, 0:16], in1=carryb[:], op=ALU.add),
                     reads=[pb2, B_carry], writes=[bcs])
                P.op("dve", lambda e, cs_=cs_, ti=ti: e.tensor_scalar(out=ncs_all[:, ti, :], in0=cs_[:], scalar1=-1.0, scalar2=None, op0=ALU.mult),
                     reads=[bcs], writes=[B_ncs])
                P.op("dve", lambda e, pt2=pt2: e.tensor_tensor(out=carryb[:], in0=carryb[:], in1=pt2[:, 16:32], op=ALU.add),
                     reads=[pb2, B_carry], writes=[B_carry])
                pt3, pb3 = bank(0, 6)
                P.op("pe", lambda e, pt3=pt3, cs_=cs_: e.transpose(out=pt3[0:16, 0:128], in_=cs_[:], identity=identf[:]), reads=[bcs, Bc], writes=[pb3])
                cT, bcT = cTr.next()
                sl = slice(i * 128, (i + 1) * 128)
                P.op("dve", lambda e, pt3=pt3, cT=cT: e.tensor_copy(out=cT[:], in_=pt3[0:16, 0:128]), reads=[pb3], writes=[bcT])
                P.op("dve", lambda e, cT=cT, sl=sl: e.tensor_copy(out=csThi[:, sl], in_=cT[:]), reads=[bcT], writes=[B_csT])
                P.op("dve", lambda e, cT=cT, sl=sl: e.tensor_tensor(out=cT[:], in0=cT[:], in1=csThi[:, sl], op=ALU.subtract), reads=[bcT, B_csT], writes=[bcT])
                P.op("dve", lambda e, cT=cT, sl=sl: e.tensor_copy(out=csTlo[:, sl], in_=cT[:]), reads=[bcT], writes=[B_csT])
            if paired:
                sfx = sbt(loc, "b1_sfx", [128, NT, 16], F32)
                bsfx = Buf("sfx")
                P.op("dve", lambda e: e.tensor_tensor(out=sfx[:], in0=ncs_all[:, NT:2 * NT, :],
                                                      in1=carryb[:].unsqueeze(1).to_broadcast([128, NT, 16]), op=ALU.add),
                     reads=[B_ncs, B_carry], writes=[bsfx])
                dma("sp", X_S, sfx[:].rearrange("p a b -> p (a b)"), [bsfx], [B_XS])
                P.op("pool", lambda e: e.collective_compute("AllGather", ALU.bypass, replica_groups=PAIRS, ins=[X_S], outs=[G_S]),
                     reads=[B_XS], writes=[B_GS], cc=True)
                dma("sp", ncs_all[:, 0:NT, :].rearrange("p a b -> p (a b)"), G_S[0:128, :], [B_GS], [B_ncs])
                P.op("dve", lambda e: e.tensor_scalar(out=ncs_all[:, 0:NT, :], in0=ncs_all[:, 0:NT, :], scalar1=role_s[:, 1:2], scalar2=None, op0=ALU.add),
                     reads=[B_ncs, Bc], writes=[B_ncs])

    def phase_fox_attn(s):
        c0 = SG if paired else s * SG
        nkt_all = (c0 + SG) // 128
        with scope() as loc:
            L = K2(loc)
            QT, bQ = load_xT(loc, "c1_Q", D_QT, B_QT, FH)
            oT = sbt(loc, "c1_oT", [128, FH, SG], BF16)
            boT = Buf("oT")
            ktr = Rot(L, "c1_kt", [128, nkt_all * 128], BF16, 2)
            vtr = Rot(L, "c1_vt", [128, nkt_all, 128], BF16, 2)
            cqr = Rot(L, "c1_cq", [128, 512], F32, 2)
            tmr = Rot(L, "c1_tm", [128, 512], F32, 5)
            pmr = Rot(L, "c1_pm", [128, 512], BF16, 7)
            rir = Rot(L, "c1_ri", [128, 512], F32, 2)
            for h in range(FH):
                kt_, bkt = ktr.next()
                vt_, bvt = vtr.next()
                if paired:
                    rs_ = slice((h % 4) * 128, (h % 4 + 1) * 128)
                    dma("sp", kt_[:, SG:2 * SG], X_K[h // 4][rs_, :], [B_XK[h // 4]], [bkt])
                    dma("sp", kt_[:, 0:SG], G_K[h // 4][rs_, :], [B_GK[h // 4]], [bkt])
                    for j in range(4):
                        dma("sp", vt_[:, NT + 2 * j:NT + 2 * j + 2, :], X_V[j][:, h * 128:(h + 1) * 128].rearrange("(t p) d -> p t d", p=128),
                            [B_XV[j]], [bvt])
                        dma("sp", vt_[:, 2 * j:2 * j + 2, :], G_V[j][0:256, h * 128:(h + 1) * 128].rearrange("(t p) d -> p t d", p=128),
                            [B_GV[j]], [bvt])
                else:
                    dma("sp", kt_[:], D_KT[h, :, 0:nkt_all * 128], [B_KT], [bkt])
                    for j0 in range(0, nkt_all, 8):
                        dma("sp", vt_[:, j0:j0 + 8, :], D_V[j0:j0 + 8, :, h * 128:(h + 1) * 128].rearrange("j p d -> p j d"), [B_V], [bvt])
                for G in range(NGRP):
                    q0 = c0 + G * 512
                    qs = slice(G * 512, (G + 1) * 512)
                    pq, pbq = ps[3], psb[3]
                    P.op("pe", lambda e, pq=pq, h=h, qs=qs: e.matmul(pq[:, :], lhsT=selF[:, h, :], rhs=csThi[:, qs], start=True, stop=False),
                         reads=[Bc, B_csT], writes=[pbq])
                    P.op("pe", lambda e, pq=pq, h=h, qs=qs: e.matmul(pq[:, :], lhsT=selF[:, h, :], rhs=csTlo[:, qs], start=False, stop=True),
                         reads=[Bc, B_csT], writes=[pbq])
                    cq, bcq = cqr.next()
                    P.op("act", lambda e, cq=cq, pq=pq: e.copy(out=cq[:], in_=pq[:, :]), reads=[pbq], writes=[bcq])
                    nkt = (q0 + 512) // 128
                    pO, pbO = ps[4 + (G % 2) * 2], psb[4 + (G % 2) * 2]
                    pR, pbR = ps[5 + (G % 2) * 2], psb[5 + (G % 2) * 2]
                    order = (list(range(NT, nkt)) + list(range(NT))) if paired else list(range(nkt))
                    stage2 = []
                    for ki, kt in enumerate(order):
                        lo_ = 0 if kt * 128 < q0 else (kt * 128 - q0)
                        cs = slice(lo_, 512)
                        pS, pbS = bank(0, 3)
                        P.op("pe", lambda e, pS=pS, kt_=kt_, kt=kt, h=h, G=G, lo_=lo_, cs=cs: e.matmul(
                            pS[:, cs], lhsT=kt_[:, kt * 128:(kt + 1) * 128], rhs=QT[:, h, G * 512 + lo_:(G + 1) * 512], start=True, stop=True),
                            reads=[bkt, bQ], writes=[pbS])
                        tm, btm = tmr.next()
                        P.op("dve", lambda e, tm=tm, pS=pS, cs=cs, kt=kt, h=h, cq=cq: e.scalar_tensor_tensor(
                            out=tm[:, cs], in0=pS[:, cs], scalar=ncs_all[:, kt, h:h + 1], in1=cq[:, cs], op0=ALU.add, op1=ALU.add),
                            reads=[pbS, B_ncs, bcq], writes=[btm])
                        pm, bpm = pmr.next()
                        P.op("act", lambda e, pm=pm, tm=tm, cs=cs: e.activation(out=pm[:, cs], in_=tm[:, cs], func=AF.Exp), reads=[btm], writes=[bpm])
                        if kt * 128 >= q0:
                            P.op("dve", lambda e, pm=pm, lo_=lo_: e.tensor_tensor(out=pm[:, lo_:lo_ + 128], in0=pm[:, lo_:lo_ + 128], in1=triub[:], op=ALU.mult),
                                 reads=[bpm, Bc], writes=[bpm])
                        def st2(pO=pO, pR=pR, vt_=vt_, kt=kt, pm=pm, bpm=bpm, cs=cs, nkt=nkt, ki=ki, pbO=pbO, pbR=pbR, bvt=bvt):
                            P.op("pe", lambda e: e.matmul(
                                pO[:, cs], lhsT=vt_[:, kt, :], rhs=pm[:, cs], start=(ki == 0), stop=(ki == nkt - 1)),
                                reads=[bvt, bpm], writes=[pbO])
                            P.op("pe", lambda e: e.matmul(
                                pR[:, cs], lhsT=onesb[:], rhs=pm[:, cs], start=(ki == 0), stop=(ki == nkt - 1)),
                                reads=[Bc, bpm], writes=[pbR])
                        stage2.append(st2)
                        while len(stage2) > ALAG:
                            stage2.pop(0)()
                    for f_ in stage2:
                        f_()
                    ri, bri = rir.next()
                    P.op("dve", lambda e, ri=ri, pR=pR: e.reciprocal(out=ri[:], in_=pR[:, :]), reads=[pbR], writes=[bri])
                    P.op("dve", lambda e, ri=ri, pO=pO, h=h, qs=qs: e.tensor_tensor(out=oT[:, h, qs], in0=pO[:, :], in1=ri[:], op=ALU.mult),
                         reads=[pbO, bri], writes=[boT])
            for k0 in range(0, FH, 8):
                dma("sp", D_oT[k0:k0 + 8, :, :].rearrange("k p t -> p k t"), oT[:, k0:k0 + 8, :], [boT], [B_oT])

    for s in range(nsg):
        phase_input(s)
    if paired:
        assert nsg == 1
        phase_input(0, src=xp_in, Ddst=D_hTp, Bdst=B_hTp)
        phase_ssd_inproj(0, prefix=True)
        phase_ssd_core(0, prefix=True, load_prev=False)
    for s in range(nsg):
        phase_ssd_inproj(s)
        phase_ssd_core(s, load_prev=(True if paired else None))
        phase_proj_post(s, "d0", D_yn, B_yn, 32, ssd_w_out, 2, 256)
        phase_ffn(s, 0)
        phase_ple(s, 0)
    for s in range(nsg):
        phase_fox_inproj(s)
        phase_fox_attn(s)
        phase_proj_post(s, "d1", D_oT, B_oT, 16, fox_w_out, 3, 512)
        phase_ffn(s, 1)
        phase_ple(s, 1)
    for s in range(nsg):
        phase_output(s)
    P.op("sp", lambda e: None, reads=[B_out])
    P.emit(st)
    st.close()
    return nc


_CACHE = {}


def _get_program(nsg):
    if nsg not in _CACHE:
        _CACHE[nsg] = build_program(nsg)
    return _CACHE[nsg]


def _fm(v):
    return np.ascontiguousarray(v.reshape(-1, 128).T)


def make_in_maps(inputs, core_batches, tok_slices):
    f = lambda a: np.ascontiguousarray(np.asarray(a, dtype=np.float32))
    vec_fm = np.stack([_fm(f(inputs[n])[i]) for n in ("norm_mix_pre", "norm_mix_post", "norm_ffn_pre", "norm_ffn_post", "ple_norm")
                       for i in range(2)], axis=1)
    ssd_nw = _fm(f(inputs["ssd_norm_w"])[0])
    cw = f(inputs["ssd_conv_w"])[0]
    conv_w = np.ascontiguousarray(cw.reshape(4, 48, 128).transpose(2, 1, 0))
    conv_b = _fm(f(inputs["ssd_conv_b"])[0])
    rep64 = np.ascontiguousarray(np.broadcast_to(
        np.stack([f(inputs["ssd_dt_bias"])[0], f(inputs["ssd_a_log"])[0], f(inputs["ssd_d"])[0]], 0)[None], (128, 3, 64)))
    rep16 = np.ascontiguousarray(np.broadcast_to(f(inputs["fox_b_f"])[0][None], (128, 16)))
    shared = {
        "vec_fm": np.ascontiguousarray(vec_fm), "ssd_nw": ssd_nw, "conv_w": conv_w, "conv_b": conv_b,
        "rep64": rep64, "rep16": rep16,
        "ssd_w_in": f(inputs["ssd_w_in"])[0], "ssd_w_out": f(inputs["ssd_w_out"])[0],
        "fox_w_in": f(inputs["fox_w_in"])[0], "fox_w_out": f(inputs["fox_w_out"])[0],
        "ffn_w_gate": f(inputs["ffn_w_gate"]), "ffn_w_up": f(inputs["ffn_w_up"]), "ffn_w_down": f(inputs["ffn_w_down"]),
        "ple_w_proj": f(inputs["ple_w_proj"]), "ple_w_gate": f(inputs["ple_w_gate"]),
    }
    x = f(inputs["x"])
    p = f(inputs["p"])
    maps = []
    for b, ts in zip(core_batches, tok_slices):
        m = dict(shared)
        m["x"] = np.ascontiguousarray(x[b, ts])
        m["p"] = np.ascontiguousarray(p[:, b, ts])
        maps.append(m)
    return maps


def make_paired_maps(inputs):
    core_batches = [c // 2 for c in range(8)]
    toks = [slice((c % 2) * SG, (c % 2 + 1) * SG) for c in range(8)]
    maps = make_in_maps(inputs, core_batches, toks)
    x = np.asarray(inputs["x"], dtype=np.float32)
    for c, m in enumerate(maps):
        m["xp"] = np.ascontiguousarray(x[c // 2, 0:SG])
        r = np.zeros((128, 2), np.float32)
        r[:, 0] = float(c % 2)
        r[:, 1] = 0.0 if c % 2 else -30000.0
        m["role"] = r
    return maps


def kernel(**inputs):
    if "paired" not in _CACHE:
        _CACHE["paired"] = build_program(1, paired=True)
    nc = _CACHE["paired"]
    maps = make_paired_maps(inputs)
    res = run_bass_kernel_spmd(nc, maps, core_ids=list(range(8)))
    out = np.empty((BATCH, SEQ, D), np.float32)
    for c in range(8):
        out[c // 2, (c % 2) * SG:(c % 2 + 1) * SG] = np.asarray(res.results[c]["out"], dtype=np.float32)
    return out
```
